# Optimizing a Trainium2 kernel written in Bass

```python
import math
import jax, jax.numpy as jnp
from jax import lax
import numpy as np

D_MODEL = 2048
BATCH = 4
SEQ = 2048
DEPTH = 1
DEC_BATCH = 128
DEC_SEQ = 8
PAST_LEN = 16384
PAGE_SIZE = 128

N_META = 16
SSD_HEADS = 32
SSD_HEAD_DIM = 64
SSD_D_INNER = SSD_HEADS * SSD_HEAD_DIM
SSD_GROUPS = 2
SSD_STATE = 128
SSD_CONV = 4
SSD_CONV_DIM = SSD_D_INNER + 2 * SSD_GROUPS * SSD_STATE
ML_HEADS = 8
ML_QK_DIM = 128
ML_V_DIM = 256
ML_D_INNER = ML_HEADS * ML_V_DIM
MIX_WIDTH = SSD_D_INNER + ML_D_INNER
D_FF = 5632
FFN_CONV = 3
CHUNK = 128
EPS = 1e-6

IN_SIZES = [SSD_D_INNER,
            SSD_CONV_DIM,
            SSD_HEADS,
            ML_HEADS * ML_QK_DIM,
            ML_HEADS * ML_QK_DIM,
            ML_D_INNER,
            ML_HEADS,
            ML_HEADS,
            ML_D_INNER]
IN_COLS = int(sum(IN_SIZES))
IN_SPLITS = [int(s) for s in np.cumsum(IN_SIZES)[:-1]]

kernel_name = "hymba_ssd_mlstm_convffn_step"


def rmsnorm(x, w):
    xf = x.astype(jnp.float32)
    r = lax.rsqrt(jnp.mean(xf * xf, axis=-1, keepdims=True) + EPS)
    return (xf * r).astype(x.dtype) * w


def causal_dwconv(x, buf, w, b):
    K = w.shape[0]
    T = x.shape[1]
    xp = jnp.concatenate([buf.astype(x.dtype), x], axis=1)
    y = b
    for j in range(K):
        y = y + xp[:, j:j + T] * w[j]
    return y, xp[:, xp.shape[1] - (K - 1):]


def to_chunks(a, L):
    Bsz, T = a.shape[0], a.shape[1]
    return jnp.moveaxis(a.reshape(Bsz, T // L, L, *a.shape[2:]), 1, 0)


def from_chunks(a):
    a = jnp.moveaxis(a, 0, 1)
    return a.reshape(a.shape[0], a.shape[1] * a.shape[2], *a.shape[3:])


def ssd_scan(x, dt, A, Bm, Cm, S0, L):
    Bsz, T, H, P = x.shape
    G, N = Bm.shape[2], Bm.shape[3]
    E = H // G
    f32 = jnp.float32
    xs = to_chunks(x.astype(f32).reshape(Bsz, T, G, E, P), L)
    dts = to_chunks(dt.astype(f32).reshape(Bsz, T, G, E), L)
    Bs = to_chunks(Bm.astype(f32), L)
    Cs = to_chunks(Cm.astype(f32), L)
    Ag = A.astype(f32).reshape(G, E)
    causal = jnp.tril(jnp.ones((L, L), dtype=bool))

    def step(S, inp):
        xc, dtc, Bc, Cc = inp
        cum = jnp.cumsum(dtc * Ag, axis=1)
        seg = cum[:, :, None] - cum[:, None, :]
        decay = jnp.exp(jnp.where(causal[None, :, :, None, None], seg, -jnp.inf))
        CB = jnp.einsum('btgn,bsgn->btsg', Cc, Bc)
        y = jnp.einsum('btsg,btsge,bsge,bsgep->btgep', CB, decay, dtc, xc)
        y = y + jnp.einsum('btgn,bgepn,btge->btgep', Cc, S, jnp.exp(cum))
        tail = jnp.exp(cum[:, -1:] - cum) * dtc
        S = S * jnp.exp(cum[:, -1])[..., None, None] + jnp.einsum('bsgn,bsge,bsgep->bgepn', Bc, tail, xc)
        return S, y

    S, ys = lax.scan(step, S0.astype(f32).reshape(Bsz, G, E, P, N), (xs, dts, Bs, Cs))
    return from_chunks(ys).reshape(Bsz, T, H, P), S.reshape(Bsz, H, P, N)


def mlstm_scan(q, k, v, ig, lf, C0, n0, m0, L):
    f32 = jnp.float32
    qs, ks, vs = (to_chunks(a.astype(f32), L) for a in (q, k, v))
    is_, fs = to_chunks(ig, L), to_chunks(lf, L)
    causal = jnp.tril(jnp.ones((L, L), dtype=bool))

    def step(carry, inp):
        Cp, npv, mp = carry
        qc, kc, vc, ic, fc = inp
        F = jnp.cumsum(fc, axis=1)
        Dm = F[:, :, None] - F[:, None, :] + ic[:, None, :]
        Dm = jnp.where(causal[None, :, :, None], Dm, -jnp.inf)
        inter = F + mp[:, None]
        m_t = jnp.maximum(jnp.max(Dm, axis=2), inter)
        W = jnp.exp(Dm - m_t[:, :, None]) * jnp.einsum('bthd,bshd->btsh', qc, kc)
        wi = jnp.exp(inter - m_t)
        num = jnp.einsum('btsh,bshv->bthv', W, vc) + wi[..., None] * jnp.einsum('bthd,bhdv->bthv', qc, Cp)
        den = jnp.sum(W, axis=2) + wi * jnp.einsum('bthd,bhd->bth', qc, npv)
        h = num / jnp.maximum(jnp.abs(den), jnp.exp(-m_t))[..., None]
        FL = F[:, -1]
        lw = FL[:, None] - F + ic
        m_new = jnp.maximum(FL + mp, jnp.max(lw, axis=1))
        sc = jnp.exp(lw - m_new[:, None])
        dec = jnp.exp(FL + mp - m_new)
        C_new = dec[..., None, None] * Cp + jnp.einsum('bsh,bshd,bshv->bhdv', sc, kc, vc)
        n_new = dec[..., None] * npv + jnp.einsum('bsh,bshd->bhd', sc, kc)
        return (C_new, n_new, m_new), h

    (C, n, m), hs = lax.scan(step, (C0.astype(f32), n0.astype(f32), m0.astype(f32)), (qs, ks, vs, is_, fs))
    return from_chunks(hs), C, n, m


def hybrid_layer(h, seg_lens, states, params):
    conv_buf, S, Cst, nst, mst, ffn_buf = states
    (norm1_w, w_in, ssd_conv_w, ssd_conv_b, ssd_dt_bias, ssd_A_log, ssd_D, ssd_norm_w,
     ml_i_bias, ml_f_bias, ml_norm_w, w_out, norm2_w, w_up, ffn_conv_w, ffn_conv_b, w_down) = params
    Bsz, T, _ = h.shape
    f32 = jnp.float32
    u = rmsnorm(h, norm1_w) @ w_in
    z, xBC, dt_raw, q, k, v, i_raw, f_raw, o_raw = jnp.split(u, IN_SPLITS, axis=-1)
    xBC, conv_new = causal_dwconv(xBC, conv_buf, ssd_conv_w, ssd_conv_b)
    xBC = jax.nn.silu(xBC)
    xs, Bm, Cm = jnp.split(xBC, [SSD_D_INNER, SSD_D_INNER + SSD_GROUPS * SSD_STATE], axis=-1)
    xs = xs.reshape(Bsz, T, SSD_HEADS, SSD_HEAD_DIM)
    Bm = Bm.reshape(Bsz, T, SSD_GROUPS, SSD_STATE)
    Cm = Cm.reshape(Bsz, T, SSD_GROUPS, SSD_STATE)
    dt = jax.nn.softplus(dt_raw.astype(f32) + ssd_dt_bias.astype(f32))
    A = -jnp.exp(ssd_A_log.astype(f32))
    q = q.reshape(Bsz, T, ML_HEADS, ML_QK_DIM)
    k = k.reshape(Bsz, T, ML_HEADS, ML_QK_DIM) * (ML_QK_DIM ** -0.5)
    v = v.reshape(Bsz, T, ML_HEADS, ML_V_DIM)
    ig = i_raw.astype(f32) + ml_i_bias.astype(f32)
    lf = jax.nn.log_sigmoid(f_raw.astype(f32) + ml_f_bias.astype(f32))
    y_ssd_parts, h_ml_parts = [], []
    start = 0
    for Lseg in seg_lens:
        sl = slice(start, start + Lseg)
        L = math.gcd(Lseg, CHUNK)
        y_seg, S = ssd_scan(xs[:, sl], dt[:, sl], A, Bm[:, sl], Cm[:, sl], S, L)
        h_seg, Cst, nst, mst = mlstm_scan(q[:, sl], k[:, sl], v[:, sl], ig[:, sl], lf[:, sl], Cst, nst, mst, L)
        y_ssd_parts.append(y_seg)
        h_ml_parts.append(h_seg)
        start += Lseg
    y_ssd = jnp.concatenate(y_ssd_parts, axis=1)
    y_ssd = (y_ssd + ssd_D.astype(f32)[:, None] * xs.astype(f32)).astype(h.dtype)
    y_ssd = rmsnorm(y_ssd.reshape(Bsz, T, SSD_D_INNER) * jax.nn.silu(z), ssd_norm_w)
    h_ml = jnp.concatenate(h_ml_parts, axis=1).astype(h.dtype)
    h_ml = rmsnorm(h_ml, ml_norm_w.reshape(ML_HEADS, ML_V_DIM)).reshape(Bsz, T, ML_D_INNER)
    y_ml = jax.nn.sigmoid(o_raw) * h_ml
    h = h + jnp.concatenate([y_ssd, y_ml], axis=-1) @ w_out
    up = rmsnorm(h, norm2_w) @ w_up
    up, ffn_new = causal_dwconv(up, ffn_buf, ffn_conv_w, ffn_conv_b)
    gate, val = jnp.split(up, 2, axis=-1)
    h = h + (jax.nn.silu(gate) * val) @ w_down
    new_states = (conv_new, S.astype(conv_new.dtype), Cst, nst, mst, ffn_new)
    return h, new_states


def setup_inputs(seed: int = 0) -> dict:
    key = jax.random.key(seed)
    ks = jax.random.split(key, 32)
    f32 = jnp.float32
    nrm = lambda kk, shape, s: jax.random.normal(kk, shape, f32) * s
    dt0 = jnp.exp(jax.random.uniform(ks[16], (DEPTH, SSD_HEADS), f32, math.log(1e-3), math.log(1e-1)))
    return {
        "x_prompt": nrm(ks[0], (BATCH, SEQ, D_MODEL), 1.0),
        "x_sample": nrm(ks[1], (DEC_BATCH, DEC_SEQ, D_MODEL), 1.0),
        "state_ssd_conv": nrm(ks[2], (DEPTH, DEC_BATCH, SSD_CONV - 1, SSD_CONV_DIM), 1.0),
        "state_ssd": nrm(ks[3], (DEPTH, DEC_BATCH, SSD_HEADS, SSD_HEAD_DIM, SSD_STATE), 0.1),
        "state_mlstm_C": nrm(ks[4], (DEPTH, DEC_BATCH, ML_HEADS, ML_QK_DIM, ML_V_DIM), 0.1),
        "state_mlstm_n": jnp.abs(nrm(ks[5], (DEPTH, DEC_BATCH, ML_HEADS, ML_QK_DIM), 0.5)),
        "state_mlstm_m": nrm(ks[6], (DEPTH, DEC_BATCH, ML_HEADS), 0.5),
        "state_ffn_conv": nrm(ks[7], (DEPTH, DEC_BATCH, FFN_CONV - 1, 2 * D_FF), 1.0),
        "meta_tokens": nrm(ks[8], (N_META, D_MODEL), 1.0),
        "norm1_w": 1.0 + nrm(ks[9], (DEPTH, D_MODEL), 0.02),
        "w_in": nrm(ks[10], (DEPTH, D_MODEL, IN_COLS), D_MODEL ** -0.5),
        "ssd_conv_w": nrm(ks[11], (DEPTH, SSD_CONV, SSD_CONV_DIM), SSD_CONV ** -0.5),
        "ssd_conv_b": nrm(ks[12], (DEPTH, SSD_CONV_DIM), 0.02),
        "ssd_dt_bias": dt0 + jnp.log(-jnp.expm1(-dt0)),
        "ssd_A_log": jnp.log(jax.random.uniform(ks[13], (DEPTH, SSD_HEADS), f32, 1.0, 16.0)),
        "ssd_D": 1.0 + nrm(ks[14], (DEPTH, SSD_HEADS), 0.02),
        "ssd_norm_w": 1.0 + nrm(ks[15], (DEPTH, SSD_D_INNER), 0.02),
        "ml_i_bias": nrm(ks[17], (DEPTH, ML_HEADS), 0.1) - 1.0,
        "ml_f_bias": jax.random.uniform(ks[18], (DEPTH, ML_HEADS), f32, 3.0, 6.0),
        "ml_norm_w": 1.0 + nrm(ks[19], (DEPTH, ML_D_INNER), 0.02),
        "w_out": nrm(ks[20], (DEPTH, MIX_WIDTH, D_MODEL), MIX_WIDTH ** -0.5),
        "norm2_w": 1.0 + nrm(ks[21], (DEPTH, D_MODEL), 0.02),
        "w_up": nrm(ks[22], (DEPTH, D_MODEL, 2 * D_FF), D_MODEL ** -0.5),
        "ffn_conv_w": nrm(ks[23], (DEPTH, FFN_CONV, 2 * D_FF), FFN_CONV ** -0.5),
        "ffn_conv_b": nrm(ks[24], (DEPTH, 2 * D_FF), 0.02),
        "w_down": nrm(ks[25], (DEPTH, D_FF, D_MODEL), D_FF ** -0.5),
        "final_norm_w": 1.0 + nrm(ks[26], (D_MODEL,), 0.02),
    }


def reference(x_prompt, x_sample, state_ssd_conv, state_ssd, state_mlstm_C, state_mlstm_n,
              state_mlstm_m, state_ffn_conv, meta_tokens, norm1_w, w_in, ssd_conv_w, ssd_conv_b,
              ssd_dt_bias, ssd_A_log, ssd_D, ssd_norm_w, ml_i_bias, ml_f_bias, ml_norm_w, w_out,
              norm2_w, w_up, ffn_conv_w, ffn_conv_b, w_down, final_norm_w):
    Bp = x_prompt.shape[0]
    f32 = jnp.float32
    meta = jnp.broadcast_to(meta_tokens[None].astype(x_prompt.dtype), (Bp, N_META, D_MODEL))
    hp = jnp.concatenate([meta, x_prompt], axis=1)
    hs = x_sample
    p_lists = [[] for _ in range(6)]
    s_lists = [[] for _ in range(6)]
    for l in range(DEPTH):
        params = (norm1_w[l], w_in[l], ssd_conv_w[l], ssd_conv_b[l], ssd_dt_bias[l], ssd_A_log[l],
                  ssd_D[l], ssd_norm_w[l], ml_i_bias[l], ml_f_bias[l], ml_norm_w[l], w_out[l],
                  norm2_w[l], w_up[l], ffn_conv_w[l], ffn_conv_b[l], w_down[l])
        p_init = (jnp.zeros((Bp, SSD_CONV - 1, SSD_CONV_DIM), hp.dtype),
                  jnp.zeros((Bp, SSD_HEADS, SSD_HEAD_DIM, SSD_STATE), f32),
                  jnp.zeros((Bp, ML_HEADS, ML_QK_DIM, ML_V_DIM), f32),
                  jnp.zeros((Bp, ML_HEADS, ML_QK_DIM), f32),
                  jnp.zeros((Bp, ML_HEADS), f32),
                  jnp.zeros((Bp, FFN_CONV - 1, 2 * D_FF), hp.dtype))
        s_init = (state_ssd_conv[l], state_ssd[l], state_mlstm_C[l], state_mlstm_n[l],
                  state_mlstm_m[l], state_ffn_conv[l])
        hp, p_new = hybrid_layer(hp, (N_META, hp.shape[1] - N_META), p_init, params)
        hs, s_new = hybrid_layer(hs, (hs.shape[1],), s_init, params)
        for j in range(6):
            p_lists[j].append(p_new[j])
            s_lists[j].append(s_new[j])
    y_prompt = rmsnorm(hp, final_norm_w)[:, N_META:]
    y_sample = rmsnorm(hs, final_norm_w)
    p_st = [jnp.stack(a, axis=0) for a in p_lists]
    s_st = [jnp.stack(a, axis=0) for a in s_lists]
    return (y_prompt, y_sample,
            p_st[0], p_st[1], p_st[2], p_st[3], p_st[4], p_st[5],
            s_st[0], s_st[1], s_st[2], s_st[3], s_st[4], s_st[5])
```

```python
import contextlib
import numpy as np
import concourse.bass as bass
import concourse.mybir as mybir
from concourse.bass_utils import run_bass_kernel_spmd

F32 = mybir.dt.float32
BF16 = mybir.dt.bfloat16
AF = mybir.ActivationFunctionType
ALU = mybir.AluOpType
AX = mybir.AxisListType

D = 2048
KC = 16
NMETA = 16
INC = 10800
Z0, XBC0, DT0, Q0, K0, V0, IF0, O0 = 0, 2048, 4608, 4640, 5664, 6688, 8736, 8752
DFF = 5632
EPS = 1e-6
NSEQ_S = 16
LS = 8

ENGS = ("pe", "act", "dve", "pool", "sp")
NDMA = {'sp': 24, 'pool': 48, 'act': 8}


class H:
    __slots__ = ("name", "w", "r", "excl")

    def __init__(self, name="", excl=False):
        self.name = name
        self.w = {}
        self.r = {}
        self.excl = excl


class Sched:
    def __init__(self, nc):
        self.nc = nc
        self.ops = {e: [] for e in ENGS}
        self.cnt = {e: 0 for e in ENGS}
        self.seen = {e: {} for e in ENGS}
        self.dma_n = {"sp": 0, "pool": 0, "act": 0}
        self.sems = {}
        self.final = []
        self.final_eng = "sp"
        self.outs = {}

    def _deps(self, reads, writes, eng=None):
        d = {}
        for h in reads:
            for k, v in h.w.items():
                if d.get(k, 0) < v:
                    d[k] = v
            if h.excl:
                for k, v in h.r.items():
                    if k != eng and d.get(k, 0) < v:
                        d[k] = v
        for h in writes:
            for k, v in h.w.items():
                if d.get(k, 0) < v:
                    d[k] = v
            for k, v in h.r.items():
                if d.get(k, 0) < v:
                    d[k] = v
        return d

    def _mark(self, reads, writes, key, val):
        for h in reads:
            if h.r.get(key, 0) < val:
                h.r[key] = val
        for h in writes:
            h.w = {key: val}
            h.r = {}

    def op(self, eng, emit, reads=(), writes=()):
        d = self._deps(reads, writes, eng)
        waits = []
        seen = self.seen[eng]
        for k, v in d.items():
            if k == eng and eng == "pe":
                continue
            if seen.get(k, 0) < v:
                seen[k] = v
                waits.append((k, v))
        self.cnt[eng] += 1
        idx = self.cnt[eng]
        self.ops[eng].append((waits, emit, (eng, 1)))
        self._mark(reads, writes, eng, idx)

    def dma(self, q, out, in_, reads=(), writes=(), is_out=False):
        d = self._deps(reads, writes)
        j = self.dma_n[q]
        self.dma_n[q] += 1
        key = ("dma", q, j % NDMA[q])
        val = 16 * (j // NDMA[q] + 1)
        if val > 16:
            d[key] = max(d.get(key, 0), val - 16)
        waits = []
        seen = self.seen[q]
        for k, v in d.items():
            if seen.get(k, 0) < v:
                seen[k] = v
                waits.append((k, v))
        self.ops[q].append((waits, (lambda e: e.dma_start(out=out, in_=in_)), (key, 16)))
        self._mark(reads, writes, key, val)
        if is_out:
            self.outs[key] = max(self.outs.get(key, 0), val)

    def alias(self, olds, news):
        w = {}
        for h in olds:
            for src in (h.w, h.r):
                for k, v in src.items():
                    if w.get(k, 0) < v:
                        w[k] = v
        for h in news:
            for k, v in w.items():
                if h.w.get(k, 0) < v:
                    h.w[k] = v

    def finish(self, eng="sp"):
        self.final = list(self.outs.items())
        self.final_eng = eng

    def emit_all(self):
        nc = self.nc
        keys = list(ENGS)
        for q in ("sp", "pool", "act"):
            if self.dma_n[q]:
                for i in range(NDMA[q]):
                    keys.append(("dma", q, i))
        with contextlib.ExitStack() as es:
            for k in keys:
                nm = k if isinstance(k, str) else "d_%s_%d" % (k[1], k[2])
                self.sems[k] = es.enter_context(nc.semaphore("s_" + nm))
            block = es.enter_context(nc.Block())
            sems = self.sems

            def run(engname):
                def f(e):
                    for waits, emit, inc in self.ops[engname]:
                        for k, v in waits:
                            e.wait_ge(sems[k], v)
                        emit(e).then_inc(sems[inc[0]], inc[1])
                    if self.final and self.final_eng == engname:
                        for k, v in self.final:
                            e.wait_ge(sems[k], v)
                return f

            block.tensor(run("pe"))
            block.scalar(run("act"))
            block.vector(run("dve"))
            block.gpsimd(run("pool"))
            block.sync(run("sp"))


class Arena:
    def __init__(self, S, tensor, size):
        self.S = S
        self.t = tensor
        self.size = size
        self.live = []

    def view(self, name, start, n):
        end = start + n
        assert end <= self.size, (name, end, self.size)
        h = H(name)
        olds = [x[2] for x in self.live if x[0] < end and start < x[1]]
        self.S.alias(olds, [h])
        self.live = [x for x in self.live if not (x[0] < end and start < x[1])]
        self.live.append((start, end, h))
        return self.t[:, start:end], h


def build_nc(pre_Ls, full_Ls, debug=False):
    nc = bass.Bass("TRN2", target_bir_lowering=False)
    S = Sched(nc)
    NPRE = sum(pre_Ls); NFULL = sum(full_Ls)

    def din(name, shape, dt=F32):
        return nc.dram_tensor(name, list(shape), dt, kind="ExternalInput").ap()

    def dout(name, shape):
        return nc.dram_tensor(name, list(shape), F32, kind="ExternalOutput").ap()

    def dscr(name, shape, dt=BF16):
        return nc.dram_tensor(name, list(shape), dt, kind="Internal").ap()

    xpre = din("xpre", [NPRE, D]); xfull = din("xfull", [NFULL, D]); xs = din("xs", [128, D]); pmask_d = din("pmask", [1])
    st_conv = din("st_conv", [48, 2560]); st_ssd = din("st_ssd", [16, 2048, 128])
    st_C = din("st_C", [16, 8, 128, 256]); st_n = din("st_n", [128, 128]); st_m = din("st_m", [16, 8])
    st_ffn = din("st_ffn", [32, 2 * DFF])
    w_in = din("w_in", [D, INC]); w_out = din("w_out", [4096, D]); w_up = din("w_up", [D, 2 * DFF])
    w_down = din("w_down", [DFF, D])
    norm1_w = din("norm1_w", [D]); norm2_w = din("norm2_w", [D]); ssd_norm_w = din("ssd_norm_w", [D])
    ml_norm_w = din("ml_norm_w", [D]); final_norm_w = din("final_norm_w", [D])
    ssd_conv_w = din("ssd_conv_w", [4, 2560]); ssd_conv_b = din("ssd_conv_b", [2560])
    ssd_dt_bias = din("ssd_dt_bias", [32]); ssd_A_log = din("ssd_A_log", [32]); ssd_D = din("ssd_D", [32])
    ml_i_bias = din("ml_i_bias", [8]); ml_f_bias = din("ml_f_bias", [8])
    ffn_conv_w = din("ffn_conv_w", [3, 2 * DFF]); ffn_conv_b = din("ffn_conv_b", [2 * DFF])

    yp = dout("yp", [NFULL, D]); ys = dout("ys", [128, D])
    p_conv = dout("p_conv", [3, 2560]); p_ssd = dout("p_ssd", [2048, 128]); p_C = dout("p_C", [8, 128, 256])
    p_n = dout("p_n", [8, 128]); p_m = dout("p_m", [1, 8]); p_ffn = dout("p_ffn", [2, 2 * DFF])
    s_conv = dout("s_conv", [48, 2560]); s_ssd = dout("s_ssd", [16, 2048, 128]); s_C = dout("s_C", [16, 8, 128, 256])
    s_n = dout("s_n", [128, 128]); s_m = dout("s_m", [16, 8]); s_ffn = dout("s_ffn", [32, 2 * DFF])

    WSCR_N = 128 * (KC * INC + 32 * D + KC * 2 * DFF + 44 * D)
    wscr = dscr("wscr", [WSCR_N])

    es = contextlib.ExitStack()
    with es:
        es.enter_context(nc.allow_non_contiguous_dma(reason="small strided parameter/state layouts"))
        _n = [0]

        def sb(shape, dt=F32, name=None):
            _n[0] += 1
            return es.enter_context(nc.sbuf_tensor(name or ("t%d" % _n[0]), list(shape), dt))

        def ps(shape, dt=F32, name=None):
            _n[0] += 1
            return es.enter_context(nc.psum_tensor(name or ("p%d" % _n[0]), list(shape), dt))

        PB = [ps([128, 512], F32, "pb%d" % i) for i in range(7)]
        HPB = [H("pb%d" % i, excl=True) for i in range(7)]
        TB = ps([128, 1024], BF16, "tb"); HTB = H("tb", excl=True)

        identf = sb([128, 128]); identb = sb([128, 128], BF16); onesf = sb([128, 128]); onesb = sb([128, 1], BF16)
        U0 = sb([128, 128]); U2 = sb([128, 128]); SS2 = sb([128, 128]); sel2 = sb([128, 16])
        negm0 = sb([128, 128], BF16); negm2 = sb([128, 128], BF16)
        blkm = sb([128, 16, 128], BF16); Eexp = sb([32, 2048]); sel2b = sb([128, 16], BF16)
        HC = H("consts")

        def pop_(f):
            S.op("pool", f, reads=[HC], writes=[HC])
        pop_(lambda e: e.memset(identf[:], 0.0))
        pop_(lambda e: e.affine_select(out=identf[:], in_=identf[:], pattern=[[-1, 128]], compare_op=ALU.not_equal,
                                       fill=1.0, base=0, channel_multiplier=1))
        pop_(lambda e: e.memset(onesf[:], 1.0))
        pop_(lambda e: e.memset(onesb[:], 1.0))
        pop_(lambda e: e.memset(U0[:], 1.0))
        pop_(lambda e: e.affine_select(out=U0[:], in_=U0[:], pattern=[[1, 128]], compare_op=ALU.is_ge,
                                       fill=0.0, base=0, channel_multiplier=-1))
        pop_(lambda e: e.memset(sel2[:], 1.0))
        pop_(lambda e: e.affine_select(out=sel2[:], in_=sel2[:], pattern=[[-8, 16]], compare_op=ALU.is_ge,
                                       fill=0.0, base=0, channel_multiplier=1))
        pop_(lambda e: e.affine_select(out=sel2[:], in_=sel2[:], pattern=[[8, 16]], compare_op=ALU.is_ge,
                                       fill=0.0, base=7, channel_multiplier=-1))
        pop_(lambda e: e.memset(blkm[:], 1.0))
        pop_(lambda e: e.affine_select(out=blkm[:].rearrange("p s (b i) -> p s b i", i=8), in_=blkm[:].rearrange("p s (b i) -> p s b i", i=8),
                                       pattern=[[1, 16], [-1, 16], [0, 8]], compare_op=ALU.is_equal,
                                       fill=0.0, base=0, channel_multiplier=0))
        pop_(lambda e: e.memset(Eexp[:], 1.0))
        pop_(lambda e: e.affine_select(out=Eexp[:].rearrange("p (h q) -> p h q", q=64), in_=Eexp[:].rearrange("p (h q) -> p h q", q=64),
                                       pattern=[[-1, 32], [0, 64]], compare_op=ALU.is_equal,
                                       fill=0.0, base=0, channel_multiplier=1))

        def dop(f):
            S.op("dve", f, reads=[HC], writes=[HC])
        dop(lambda e: e.tensor_copy(out=identb[:], in_=identf[:]))
        dop(lambda e: e.tensor_copy(out=sel2b[:], in_=sel2[:]))
        dop(lambda e: e.tensor_copy(out=SS2[:].rearrange("p (b i) -> p b i", i=8), in_=sel2[:].unsqueeze(2).to_broadcast([128, 16, 8])))
        dop(lambda e: e.tensor_tensor(out=U2[:], in0=U0[:], in1=SS2[:], op=ALU.mult))
        dop(lambda e: e.tensor_scalar(out=negm0[:], in0=U0[:], scalar1=1.0, scalar2=30000.0, op0=ALU.subtract, op1=ALU.mult))
        dop(lambda e: e.tensor_scalar(out=negm2[:], in0=U2[:], scalar1=1.0, scalar2=30000.0, op0=ALU.subtract, op1=ALU.mult))

        class Cfg:
            pass
        cfgs = {}

        def get_cfg(name, L):
            key = (name, L)
            if key in cfgs:
                return cfgs[key]
            c = Cfg()
            if name == "samp":
                c.name, c.L, c.nseq, c.Lseq = "samp", 128, 16, 8
                U, SSm, sel, negm = U2, SS2, sel2, negm2
            else:
                c.name, c.L, c.nseq, c.Lseq = "chunk", L, 1, L
                U, SSm, sel, negm = U0, onesf, onesf, negm0
            L_ = c.L
            c.U = U[0:L_, 0:L_]; c.SS = SSm[0:L_, 0:L_]; c.sel = sel[0:L_, 0:c.nseq]
            c.negm = negm[0:L_, 0:L_].unsqueeze(1).to_broadcast([L_, 4, L_])
            c.negm1 = negm[0:L_, 0:L_]
            cfgs[key] = c
            return c

        n1w = sb([128, KC]); n2w = sb([128, KC]); snw = sb([128, KC]); mnw = sb([128, KC])
        dtb = sb([128, 32]); Aneg = sb([128, 32]); Dbc = sb([128, 32]); ibb = sb([128, 8]); fbb = sb([128, 8])
        cw = sb([128, 20, 4]); cbi = sb([128, 20]); fw = sb([128, 88, 3]); fbi = sb([128, 88])
        HP = H("params")
        WSRC = {"in": w_in.rearrange("(kc p) c -> p kc c", p=128),
                "out": w_out.rearrange("(kc p) c -> p kc c", p=128),
                "up": w_up.rearrange("(kc p) c -> p kc c", p=128),
                "dn": w_down.rearrange("(kc p) c -> p kc c", p=128)}
        BLOCKS = []
        for wb in range(5):
            BLOCKS.append(("in", XBC0 + wb * 512, 512, 0, KC))
        BLOCKS.append(("in", DT0, 32, 0, KC))
        for wb in range(4):
            BLOCKS.append(("in", Z0 + wb * 512, 512, 0, KC))
        for c00 in (Q0, K0):
            for wb in range(2):
                BLOCKS.append(("in", c00 + wb * 512, 512, 0, KC))
        for wb in range(4):
            BLOCKS.append(("in", V0 + wb * 512, 512, 0, KC))
        BLOCKS.append(("in", IF0, 16, 0, KC))
        for wb in range(4):
            BLOCKS.append(("in", O0 + wb * 512, 512, 0, KC))
        for cb_i in range(4):
            BLOCKS.append(("out", cb_i * 512, 512, 0, 16))
            BLOCKS.append(("out", cb_i * 512, 512, 16, 16))
        for jb in range(22):
            BLOCKS.append(("up", jb * 256, 512, 0, KC))
        for cb_i in range(4):
            for kg in range(4):
                BLOCKS.append(("dn", cb_i * 512, 512, kg * 11, 11))
        HBLK = {b: H("wblk") for b in BLOCKS}
        BV = {}
        off_ = 0
        for b in BLOCKS:
            n_ = 128 * b[4] * b[2]
            BV[b] = wscr[off_:off_ + n_].rearrange("(p k c) -> p k c", p=128, k=b[4])
            off_ += n_
        assert off_ <= WSCR_N
        cast_ptr = [0]
        cur_ti = [0]
        LOOKAHEAD = 6

        def ensure_cast(upto):
            while cast_ptr[0] < min(upto, len(BLOCKS)):
                b = BLOCKS[cast_ptr[0]]
                cast_ptr[0] += 1
                which, c0, width, k0, nk = b
                src = WSRC[which]
                dst = BV[b]
                for ka in range(0, nk, 8):
                    kb = min(nk, ka + 8)
                    if which == "up":
                        S.dma("pool", dst[:, ka:kb, 0:256], src[:, k0 + ka:k0 + kb, c0:c0 + 256], writes=[HBLK[b]])
                        S.dma("pool", dst[:, ka:kb, 256:512], src[:, k0 + ka:k0 + kb, DFF + c0:DFF + c0 + 256], writes=[HBLK[b]])
                    else:
                        S.dma("pool", dst[:, ka:kb, :], src[:, k0 + ka:k0 + kb, c0:c0 + width], writes=[HBLK[b]])

        pre_subs = []
        o_ = 0
        for L_ in pre_Ls:
            pre_subs.append(dict(cfg=get_cfg("chunk", L_), src=xpre[o_:o_ + L_, :], dst=None, kind="chunk", sonly=True))
            o_ += L_
        full_subs = []
        o_ = 0
        for i, L_ in enumerate(full_Ls):
            full_subs.append(dict(cfg=get_cfg("chunk", L_), src=xfull[o_:o_ + L_, :], dst=yp[o_:o_ + L_, :], kind="chunk",
                                  sonly=False, last=(i == len(full_Ls) - 1)))
            o_ += L_
        samp_sub = dict(cfg=get_cfg("samp", 128), src=xs[:, :], dst=ys[:, :], kind="samp", sonly=False)
        tiles = [pre_subs[i:i + 3] for i in range(0, len(pre_subs), 3)]
        n_pre_tiles = len(tiles)
        rest = full_subs[:-2]
        sizes = {7: (3, 2, 2), 1: (1,), 4: (2, 2), 0: ()}.get(len(rest))
        if sizes is None:
            sizes = tuple([3] * (len(rest) // 3) + ([len(rest) % 3] if len(rest) % 3 else []))
        o_ = 0
        for n_ in sizes:
            tiles.append(rest[o_:o_ + n_])
            o_ += n_
        tiles.append(full_subs[-2:] + [samp_sub])
        TMAX = 384

        xnT = sb([128, KC, TMAX], BF16, "xnT"); HXN = H("xnT")
        BIG = [sb([128, D], F32, "big%d" % i) for i in range(3)]
        HBIG = [H("big%d" % i) for i in range(3)]
        ARN = 36096
        arena_t = sb([128, ARN], BF16, "arena")
        AR = Arena(S, arena_t, ARN)
        WSL = [sb([128, KC, 512], BF16, "wslot%d" % i) for i in range(2)]
        HWSL = [H("wslot%d" % i) for i in range(2)]
        wctr = [0]
        Sst = sb([128, 16, 128], F32, "Sst"); HSg = [H("Sst0"), H("Sst1")]
        Cst = sb([128, 8, 256], F32, "Cst"); HCh = [H("Cst0"), H("Cst1")]
        nst = sb([128, 8], F32, "nst"); nsc = sb([128, 8], BF16, "nsc"); Hn = H("nst"); Hnsc = H("nsc")
        mstT = sb([8, 1], F32, "mstT"); Hm = H("mstT")
        nall = sb([128, 16, 8], F32, "nall"); nnew = sb([128, 16, 8], F32, "nnew"); Hnall = H("nall")
        mall = sb([8, 16], F32, "mall"); mnewT = sb([8, 16], F32, "mnewT"); Hmall = H("mall")
        halo_p = sb([128, 20, 3], F32, "halo_p"); Hhp = H("halo_p")
        halo_s = sb([128, 20, 48], F32, "halo_s"); Hhs = H("halo_s")
        fhalo_p = sb([128, 88, 2], F32, "fhalo_p"); Hfp = H("fhalo_p")
        CBW = 3 + 384 + 16 * 11
        cbuf = [sb([128, CBW], F32, "cbuf%d" % i) for i in range(2)]; Hcbuf = [H("cbuf0"), H("cbuf1")]
        cacc = [sb([128, TMAX], F32, "cacc%d" % i) for i in range(2)]; Hcacc = [H("cacc0"), H("cacc1")]
        etmp = [sb([128, 512], F32, "etmp%d" % i) for i in range(2)]; Hetmp = [H("etmp0"), H("etmp1")]
        ectr = [0]
        hres = [sb([128, 512], F32, "hres%d" % i) for i in range(1)]; Hhres = [H("hres0")]
        crow = sb([64, 512], F32, "crow"); Hcrow = H("crow")
        xsel = sb([128, KC, 64], BF16, "xsel"); Hxsel = H("xsel")
        sm = sb([128, 16], F32, "smalls"); Hsm = H("smalls")
        smT = sb([32, 640], F32, "smallsT"); HsmT = H("smallsT")
        decS = sb([128, 16, 16], F32, "decS"); HdecS = H("decS")
        decbc = sb([128, 8, 16], F32, "decbc"); Hdecbc = H("decbc")
        CB = sb([128, 2, 128], F32, "CB"); HCB = H("CB")
        nupd = sb([128, 8, 16], F32, "nupd"); Hnupd = H("nupd")
        pmk = sb([128, 1], F32, "pmk"); Hpmk = H("pmk")
        sms = [sb([128, 384], F32, "sms%d" % i) for i in range(3)]; Hsms = [H("sms%d" % i) for i in range(3)]
        AV = {}

        def aview(name, start, n, dt=BF16):
            ap, h = AR.view(name, start, n)
            if dt == F32:
                ap = ap.bitcast(F32)
            AV[name] = (ap, h)
            return ap, h
        JUNK0 = 12288 + 21760

        S.op("pool", lambda e: e.memset(Sst[:], 0.0), writes=HSg)
        S.op("pool", lambda e: e.memset(Cst[:], 0.0), writes=HCh)
        S.op("pool", lambda e: (e.memset(nst[:], 0.0), e.memset(mstT[:], 0.0))[1], writes=[Hn, Hm])
        S.op("pool", lambda e: (e.memset(halo_p[:], 0.0), e.memset(fhalo_p[:], 0.0))[1], writes=[Hhp, Hfp])

        def wload(which, c0, width, k0, nk):
            b = (which, c0, width, k0, nk)
            extra = 3 if 1 <= cur_ti[0] < n_pre_tiles else 0
            ensure_cast(max(BLOCKS.index(b) + 1 + LOOKAHEAD, cast_ptr[0] + extra))
            i = wctr[0] % 2
            wctr[0] += 1
            if width == 512:
                S.dma("sp", WSL[i][:, 0:nk, :], BV[b], reads=[HBLK[b]], writes=[HWSL[i]])
            else:
                S.dma("sp", WSL[i][:, 0:nk, 0:width], BV[b], reads=[HBLK[b]], writes=[HWSL[i]])
            return WSL[i], HWSL[i]

        def wload2_unused(ba, bb):
            ensure_cast(BLOCKS.index(bb) + 1 + LOOKAHEAD)
            i = wctr[0] % 2
            wctr[0] += 1
            for half, b in enumerate((ba, bb)):
                which, c0, width, k0, nk = b
                S.dma("sp", WSL[i][:, 0:nk, half * 256:half * 256 + 256], WSRC[which][0][:, k0:k0 + nk, c0:c0 + width],
                      reads=[HBLK[b]], writes=[HWSL[i]])
            return WSL[i], HWSL[i]

        def dbg(name, ap, h, L):
            return

        def dbgstage(stage, si, s_):
            if debug == stage and s_["dst"] is not None:
                L_ = s_["cfg"].L
                S.dma("pool", s_["dst"], BIG[si][0:L_, :], reads=[HBIG[si]], is_out=True)

        def mm(out, lhsT, rhs, start, stop, reads, writes):
            S.op("pe", lambda e: e.matmul(out, lhsT=lhsT, rhs=rhs, start=start, stop=stop), reads=reads, writes=writes)

        def tr(out, in_, ident, reads, writes):
            S.op("pe", lambda e: e.transpose(out, in_, ident), reads=reads, writes=writes)

        def act(out, in_, func, reads, writes, bias=None, scale=None, accum=None):
            kw = {}
            if bias is not None:
                kw["bias"] = bias
            if scale is not None:
                kw["scale"] = scale
            if accum is not None:
                kw["accum_out"] = accum
            S.op("act", lambda e: e.activation(out=out, in_=in_, func=func, **kw), reads=reads, writes=writes)

        def tt(eng, out, in0, in1, op, reads, writes):
            S.op(eng, lambda e: e.tensor_tensor(out=out, in0=in0, in1=in1, op=op), reads=reads, writes=writes)

        def ts(eng, out, in0, s1, op0, reads, writes, s2=None, op1=None):
            if op1 is None:
                S.op(eng, lambda e: e.tensor_scalar(out=out, in0=in0, scalar1=s1, scalar2=None, op0=op0), reads=reads, writes=writes)
            else:
                S.op(eng, lambda e: e.tensor_scalar(out=out, in0=in0, scalar1=s1, scalar2=s2, op0=op0, op1=op1), reads=reads, writes=writes)

        def stt(eng, out, in0, scalar, in1, op0, op1, reads, writes):
            S.op(eng, lambda e: e.scalar_tensor_tensor(out=out, in0=in0, scalar=scalar, in1=in1, op0=op0, op1=op1),
                 reads=reads, writes=writes)

        def cp(eng, out, in_, reads, writes):
            if eng == "act":
                S.op("act", lambda e: e.activation(out=out, in_=in_, func=AF.Copy), reads=reads, writes=writes)
            else:
                S.op(eng, lambda e: e.tensor_copy(out=out, in_=in_), reads=reads, writes=writes)

        def rsqrt_mean(dst, ss, n, L, reads_writes):
            act(dst, ss, AF.Sqrt, reads_writes, reads_writes, bias=EPS, scale=1.0 / n)
            S.op("dve", lambda e: e.reciprocal(out=dst, in_=dst), reads=reads_writes, writes=reads_writes)

        def norm_T(st, L, off, wT, dstT, hdst, prescaled=False, scaled_dst=None):
            b = BIG[st]; hb = HBIG[st]
            src_ = b; hsrc = hb
            if not prescaled:
                S.op("dve", lambda e: e.memset(sm[0:L, 0:1], 0.0), writes=[Hsm])
                junk, Hjunk = AR.view("junk", JUNK0, 2048)
                act(junk[0:L, :], b[0:L, :], AF.Square, [hb, Hsm], [Hjunk, Hsm], accum=sm[0:L, 0:1])
                rsqrt_mean(sm[0:L, 1:2], sm[0:L, 0:1], D, L, [Hsm])
                if scaled_dst is None:
                    ts("dve", b[0:L, :], b[0:L, :], sm[0:L, 1:2], ALU.mult, [hb, Hsm], [hb])
                else:
                    src_, hsrc = scaled_dst
                    ts("dve", src_[0:L, :], b[0:L, :], sm[0:L, 1:2], ALU.mult, [hb, Hsm], [hsrc])
            for q in range(4):
                pb = PB[q % 2]; hpb = HPB[q % 2]
                for j in range(4):
                    kc = 4 * q + j
                    tr(pb[:, j * 128:j * 128 + L], src_[0:L, kc * 128:(kc + 1) * 128], identf[0:L, 0:L], [hsrc, HC], [hpb])
                tt("dve", dstT[:, 4 * q:4 * q + 4, off:off + L], pb[:].rearrange("p (j l) -> p j l", j=4)[:, :, 0:L],
                   wT[:, 4 * q:4 * q + 4].unsqueeze(2).to_broadcast([128, 4, L]), ALU.mult, [hpb, HP], [hdst])

        pstage = sb([128, 128], F32, "pstage"); Hpst = H("pstage")

        def load_T(dst, src1d, nb):
            S.dma("sp", pstage[0:nb, :], src1d.rearrange("(b p) -> b p", p=128), writes=[Hpst])
            tr(PB[0][:, 0:nb], pstage[0:nb, :], identf[0:nb, 0:nb], [Hpst, HC], [HPB[0]])
            cp("dve", dst, PB[0][:, 0:nb], [HPB[0]], [HP])
        for dst, src in ((n1w, norm1_w), (n2w, norm2_w), (snw, ssd_norm_w), (mnw, ml_norm_w)):
            load_T(dst[:, :], src, KC)
        for j in range(4):
            load_T(cw[:, :, j], ssd_conv_w[j], 20)
        load_T(cbi[:, :], ssd_conv_b, 20)
        for j in range(3):
            load_T(fw[:, :, j], ffn_conv_w[j], 88)
        load_T(fbi[:, :], ffn_conv_b, 88)
        for dst, src in ((dtb, ssd_dt_bias), (Aneg, ssd_A_log), (Dbc, ssd_D), (ibb, ml_i_bias), (fbb, ml_f_bias)):
            S.dma("sp", dst[:], src.partition_broadcast(128), writes=[HP])
        S.op("act", lambda e: e.activation(out=Aneg[:], in_=Aneg[:], func=AF.Exp), reads=[HP], writes=[HP])
        S.op("dve", lambda e: e.tensor_scalar(out=Aneg[:], in0=Aneg[:], scalar1=-1.0, scalar2=None, op0=ALU.mult),
             reads=[HP], writes=[HP])

        SOFF = 3 + 384

        def conv_block(pacc, hpacc, K, w_ap, b_ap, hp_ap, hhp, hs_ap, hhs, ci, T, Tp, has_samp, save_halo):
            cb_ = cbuf[ci]; hcb = Hcbuf[ci]; ac = cacc[ci]; hac = Hcacc[ci]
            Hh = K - 1
            if Tp > 0:
                cp("dve", cb_[:, Hh:Hh + Tp], pacc[:, 0:Tp], [hpacc], [hcb])
                cp("act", cb_[:, 0:Hh], hp_ap, [hhp], [hcb])
                if save_halo:
                    cp("act", hp_ap, cb_[:, Tp:Tp + Hh], [hcb], [hhp])
                act(ac[:, 0:Tp], pacc[:, 0:Tp], AF.Identity, [hpacc, HP], [hac], bias=b_ap, scale=w_ap[:, Hh:Hh + 1])
                for j in range(Hh - 1, -1, -1):
                    stt("dve", ac[:, 0:Tp], cb_[:, j:j + Tp], w_ap[:, j:j + 1], ac[:, 0:Tp], ALU.mult, ALU.add, [hcb, HP, hac], [hac])
            if has_samp:
                W = Hh + LS
                cbs = cb_[:, SOFF:SOFF + 16 * W].rearrange("p (s t) -> p s t", t=W)
                acs = ac[:, Tp:Tp + 128].rearrange("p (s t) -> p s t", t=LS)
                cp("dve", cbs[:, :, Hh:W], pacc[:, Tp:Tp + 128].rearrange("p (s t) -> p s t", t=LS), [hpacc], [hcb])
                cp("act", cbs[:, :, 0:Hh], hs_ap.rearrange("p (s t) -> p s t", t=Hh), [hhs], [hcb])
                act(acs, pacc[:, Tp:Tp + 128].rearrange("p (s t) -> p s t", t=LS), AF.Identity, [hpacc, HP], [hac], bias=b_ap, scale=w_ap[:, Hh:Hh + 1])
                for j in range(Hh - 1, -1, -1):
                    stt("dve", acs, cbs[:, :, j:j + LS], w_ap[:, j:j + 1], acs, ALU.mult, ALU.add, [hcb, HP, hac], [hac])
            return ac, hac

        for ti, tile in enumerate(tiles):
            last_tile = (ti == len(tiles) - 1)
            cur_ti[0] = ti
            offs = []
            o = 0
            for s_ in tile:
                offs.append(o)
                o += s_["cfg"].L
            T = o
            has_samp = tile[-1]["kind"] == "samp"
            Tp = T - 128 if has_samp else T
            sonly = bool(tile[0].get("sonly", False))
            base = 32 * TMAX
            ygT_ap, HYG = AR.view("ygT", 0, 32 * TMAX)
            ygT = ygT_ap.rearrange("p (k t) -> p k t", k=32)
            xtok_ap, HXT = AR.view("x_tok", base, 3 * 2048)
            x_tok = xtok_ap.rearrange("p (s c) -> p s c", s=3)
            xTall_ap, HXTA = AR.view("xT_all", base + 6144, 16 * TMAX)
            xT_all = xTall_ap.rearrange("p (b t) -> p b t", b=16)
            bct_ap, HBCT = AR.view("BCT", base + 12288, 4 * TMAX)
            BCT = bct_ap.rearrange("p (b t) -> p b t", b=4)
            btok_ap, HBT = AR.view("B_tok", base + 13824, 3 * 256)
            B_tok = btok_ap.rearrange("p (s c) -> p s c", s=3)
            xdt, HXDT = AR.view("xdt", base + 14592, 2048)
            xtail, HXTL = AR.view("xtail", base + 16640, 2048)
            mt4_ap, HMT4 = AR.view("MT4", base + 18688, 512)
            MT4 = mt4_ap.rearrange("p (j t) -> p j t", j=4)
            cmt_ap, HCMT = AR.view("CmT", base + 19200, 256)
            CmT = cmt_ap.rearrange("p (g t) -> p g t", g=2)
            bm_ap, HBM = AR.view("Bm", base + 19456, 256)
            Bm = bm_ap.rearrange("p (g t) -> p g t", g=2)
            S0T, HS0T = AR.view("S0T", base + 19712, 2048)

            for si, s_ in enumerate(tile):
                L = s_["cfg"].L
                S.dma("sp", BIG[si][0:L, :], s_["src"], writes=[HBIG[si]])
                norm_T(si, L, offs[si], n1w, xnT, HXN)

            if last_tile:
                for q in range(5):
                    et = etmp[q % 2]; het = Hetmp[q % 2]
                    S.dma("sp", et[0:48, :], st_conv[:, q * 512:(q + 1) * 512], writes=[het])
                    for j in range(4):
                        tr(PB[2][:, j * 128:j * 128 + 48], et[0:48, j * 128:(j + 1) * 128], identf[0:48, 0:48], [het, HC], [HPB[2]])
                    cp("dve", halo_s[:, 4 * q:4 * q + 4, :], PB[2][:].rearrange("p (j l) -> p j l", j=4)[:, :, 0:48], [HPB[2]], [Hhs])
                S.dma("sp", etmp[0][:, 0:128], st_n[:, :], writes=[Hetmp[0]])
                tr(PB[2][:, 0:128], etmp[0][:, 0:128], identf[:, :], [Hetmp[0], HC], [HPB[2]])
                cp("dve", nall[:].rearrange("p s h -> p (s h)"), PB[2][:, 0:128], [HPB[2]], [Hnall])
                S.dma("sp", pstage[0:16, 0:8], st_m[:, :], writes=[Hpst])
                tr(PB[2][0:8, 0:16], pstage[0:16, 0:8], identf[0:16, 0:16], [Hpst, HC], [HPB[2]])
                cp("dve", mall[:, :], PB[2][0:8, 0:16], [HPB[2]], [Hmall])
                offp = offs[1]; offS = offs[2]
                cp("dve", xsel[:, :, 0:3], xnT[:, :, offp + 125:offp + 128], [HXN], [Hxsel])
                for kc in range(KC):
                    cp("dve", xsel[:, kc, 3:51].rearrange("p (s t) -> p s t", t=3),
                       xnT[:, kc, offS:offS + 128].rearrange("p (s t) -> p s t", t=LS)[:, :, 5:8], [HXN], [Hxsel])

            for wb in range(5):
                wt, hw = wload("in", XBC0 + wb * 512, 512, 0, KC)
                for sbk in range(4):
                    blk = wb * 4 + sbk
                    pacc = PB[2 + blk % 3]; hpacc = HPB[2 + blk % 3]
                    for kc in range(KC):
                        mm(pacc[:, 0:T], wt[:, kc, sbk * 128:(sbk + 1) * 128], xnT[:, kc, 0:T], kc == 0, kc == KC - 1, [hw, HXN], [hpacc])
                    ac, hac = conv_block(pacc, hpacc, 4, cw[:, blk, :], cbi[:, blk:blk + 1], halo_p[:, blk, :], Hhp,
                                         halo_s[:, blk, :], Hhs, blk % 2, T, Tp, has_samp, not last_tile)
                    if blk < 16:
                        act(xT_all[:, blk, 0:T], ac[:, 0:T], AF.Silu, [hac], [HXTA])
                    else:
                        act(BCT[:, blk - 16, 0:T], ac[:, 0:T], AF.Silu, [hac], [HBCT])
                if last_tile:
                    for kc in range(KC):
                        mm(PB[5][0:51, :], xsel[:, kc, 0:51], wt[:, kc, :], kc == 0, kc == KC - 1, [hw, Hxsel], [HPB[5]])
                    cp("act", crow[0:51, :], PB[5][0:51, :], [HPB[5]], [Hcrow])
                    for r_ in range(3):
                        S.dma("pool", p_conv[r_:r_ + 1, wb * 512:(wb + 1) * 512], crow[r_:r_ + 1, :], reads=[Hcrow], is_out=True)
                    S.dma("pool", s_conv[:, wb * 512:(wb + 1) * 512], crow[3:51, :], reads=[Hcrow], is_out=True)
            for si, s_ in enumerate(tile):
                L = s_["cfg"].L; off = offs[si]
                for half in range(2):
                    for j in range(8):
                        tr(TB[0:L, j * 128:(j + 1) * 128], xT_all[:, half * 8 + j, off:off + L], identb[:, :], [HXTA, HC], [HTB])
                    cp("dve", x_tok[0:L, si, half * 1024:(half + 1) * 1024], TB[0:L, :], [HTB], [HXT])
                for g in range(2):
                    tr(TB[0:L, g * 128:(g + 1) * 128], BCT[:, g, off:off + L], identb[:, :], [HBCT, HC], [HTB])
                cp("dve", B_tok[0:L, si, :], TB[0:L, 0:256], [HTB], [HBT])

            ptmp_ap, Hptmp = AR.view("ptmp", base + 6144, 1024)
            ptmp = ptmp_ap.bitcast(F32)
            mt4b_ap, HMT4b = AR.view("MT4b", base + 7168, 512)
            MT4b = mt4b_ap.rearrange("p (j t) -> p j t", j=4)
            wt, hw = wload("in", DT0, 32, 0, KC)
            for si, s_ in enumerate(tile):
                L = s_["cfg"].L; off = offs[si]
                for kc in range(KC):
                    mm(PB[si][0:L, 0:32], xnT[:, kc, off:off + L], wt[:, kc, 0:32], kc == 0, kc == KC - 1, [hw, HXN], [HPB[si]])
                tt("dve", sms[si][0:L, 0:32], PB[si][0:L, 0:32], dtb[0:L, :], ALU.add, [HPB[si], HP], [Hsms[si]])

            for si, s_ in enumerate(tile):
                c = s_["cfg"]; L = c.L; nseq = c.nseq; off = offs[si]
                samp = s_["kind"] == "samp"
                m_ = sms[si]; hm = Hsms[si]
                xx = m_[0:L, 0:32]; na = m_[0:L, 32:64]; l1 = m_[0:L, 64:96]; dt = m_[0:L, 96:128]; a_ = m_[0:L, 128:160]
                cum = m_[0:L, 160:192]; ncum = m_[0:L, 192:224]; ecum = m_[0:L, 224:256]; tail = m_[0:L, 256:288]
                stt("dve", na, xx, -1.0, xx, ALU.mult, ALU.min, [hm], [hm])
                act(na, na, AF.Exp, [hm], [hm])
                act(l1, na, AF.Ln, [hm], [hm], bias=1.0)
                stt("dve", dt, xx, 0.0, l1, ALU.max, ALU.add, [hm], [hm])
                tt("dve", a_, dt, Aneg[0:L, :], ALU.mult, [hm, HP], [hm])
                mm(PB[0][0:L, 0:32], c.U, a_, True, True, [hm, HC], [HPB[0]])
                mm(PB[0][0:L, 32:64], c.SS, a_, True, True, [hm, HC], [HPB[0]])
                mm(PB[0][0:32, 64:64 + nseq], a_, c.sel, True, True, [hm, HC], [HPB[0]])
                cp("dve", cum, PB[0][0:L, 0:32], [HPB[0]], [hm])
                ts("dve", ncum, cum, -1.0, ALU.mult, [hm], [hm])
                act(ecum, PB[0][0:L, 0:32], AF.Exp, [HPB[0]], [hm])
                tt("dve", tail, PB[0][0:L, 32:64], cum, ALU.subtract, [HPB[0], hm], [hm])
                act(tail, tail, AF.Exp, [hm], [hm])
                tt("dve", tail, tail, dt, ALU.mult, [hm], [hm])
                etot = smT[0:32, 0:nseq]
                act(etot, PB[0][0:32, 64:64 + nseq], AF.Exp, [HPB[0]], [HsmT])
                for j in range(16):
                    mm(PB[1][:, j * nseq:(j + 1) * nseq], Eexp[:, j * 128:(j + 1) * 128], etot, True, True, [HC, HsmT], [HPB[1]])
                cp("dve", decS[:, :, 0:nseq], PB[1][:, 0:16 * nseq].rearrange("p (j s) -> p j s", s=nseq), [HPB[1]], [HdecS])
                if not sonly:
                    tt("dve", xdt[0:L, :].rearrange("p (h q) -> p h q", q=64), x_tok[0:L, si, :].rearrange("p (h q) -> p h q", q=64),
                       dt.unsqueeze(2).to_broadcast([L, 32, 64]), ALU.mult, [HXT, hm], [HXDT])
                tt("dve", xtail[0:L, :].rearrange("p (h q) -> p h q", q=64), x_tok[0:L, si, :].rearrange("p (h q) -> p h q", q=64),
                   tail.unsqueeze(2).to_broadcast([L, 32, 64]), ALU.mult, [HXT, hm], [HXTL])
                if not sonly:
                    for g in range(2):
                        mm(PB[2][0:L, g * 128:g * 128 + L], BCT[:, g, off:off + L], BCT[:, 2 + g, off:off + L], True, True, [HBCT], [HPB[2]])
                    cp("act", CB[0:L, :, 0:L], PB[2][0:L, 0:256].rearrange("p (g t) -> p g t", g=2)[:, :, 0:L], [HPB[2]], [HCB])
                    PBC = ((PB[3], HPB[3]), (PB[6], HPB[6]))
                    MTB = ((MT4, HMT4), (MT4b, HMT4b))

                    def px_bcast(hq):
                        pbc, hpbc = PBC[hq % 2]
                        if L == 128:
                            mm(pbc[0:L, :].rearrange("p (j t) -> p j t", j=4)[:, :, 0:L], identb[0:L, 0:L], c.negm, True, False, [HC], [hpbc])
                        else:
                            for j in range(4):
                                mm(pbc[0:L, j * 128:j * 128 + L], identb[0:L, 0:L], c.negm1, j == 0, False, [HC], [hpbc])
                        for j in range(4):
                            h = 4 * hq + j
                            mm(pbc[0:L, j * 128:j * 128 + L], a_[:, h:h + 1].to_broadcast([L, L]), c.U, False, j == 3, [hm, HC], [hpbc])

                    def px_elem(hq):
                        pbc, hpbc = PBC[hq % 2]
                        mt, hmt = MTB[hq % 2]
                        g = hq // 4
                        et = etmp[ectr[0] % 2]; het = Hetmp[ectr[0] % 2]; ectr[0] += 1
                        for j in range(4):
                            h = 4 * hq + j
                            act(et[0:L, j * 128:j * 128 + L], pbc[0:L, j * 128:j * 128 + L], AF.Exp, [hpbc, hm], [het], bias=ncum[:, h:h + 1])
                        tt("dve", mt[0:L, :, 0:L], et[0:L, :].rearrange("p (j t) -> p j t", j=4)[:, :, 0:L],
                           CB[0:L, g, 0:L].unsqueeze(1).to_broadcast([L, 4, L]), ALU.mult, [het, HCB], [hmt])

                    def px_py(hq):
                        mt, hmt = MTB[hq % 2]
                        for j in range(4):
                            h = 4 * hq + j
                            hh = h % 16
                            py = PB[4 + hh // 8]; hpy = HPB[4 + hh // 8]
                            mm(py[0:L, (hh % 8) * 64:(hh % 8) * 64 + 64], mt[0:L, j, 0:L], xdt[0:L, h * 64:(h + 1) * 64], True, True, [hmt, HXDT], [hpy])

                    def px_evac(g):
                        for b2 in range(2):
                            cols = slice(g * 1024 + b2 * 512, g * 1024 + b2 * 512 + 512)
                            cp("act", BIG[si][0:L, cols], PB[4 + b2][0:L, :], [HPB[4 + b2]], [HBIG[si]])
                            h0 = g * 16 + b2 * 8
                            tt("dve", ptmp[0:L, :].rearrange("p (h q) -> p h q", q=64), x_tok[0:L, si, cols].rearrange("p (h q) -> p h q", q=64),
                               Dbc[0:L, h0:h0 + 8].unsqueeze(2).to_broadcast([L, 8, 64]), ALU.mult, [HXT, HP], [Hptmp])
                            tt("dve", BIG[si][0:L, cols], BIG[si][0:L, cols], ptmp[0:L, :], ALU.add, [HBIG[si], Hptmp], [HBIG[si]])

                    px_bcast(0)
                    for hq in range(8):
                        if hq + 1 < 8:
                            px_bcast(hq + 1)
                        px_elem(hq)
                        px_py(hq)
                        if hq % 4 == 3:
                            px_evac(hq // 4)
                PYI = ((PB[4], HPB[4]), (PB[5], HPB[5]), (PB[2], HPB[2]), (PB[3], HPB[3]))
                if samp:
                    wsl_i = wctr[0] % 2
                    HSalt = [H("Salt0"), H("Salt1")]
                    S.alias([HWSL[wsl_i]], HSalt)
                    Salt = WSL[wsl_i][:].rearrange("p k c -> p (k c)")[:, 0:4096].bitcast(F32).rearrange("p (j n) -> p j n", j=16)
                def ssd_loads(sq):
                    Sb_, HSb_ = (Salt, HSalt) if sq % 2 == 1 else (Sst, HSg)
                    for g_ in range(2):
                        S.dma("sp", Sb_[:, g_ * 8:g_ * 8 + 8, :], st_ssd[sq, g_ * 1024:(g_ + 1) * 1024, :].rearrange("(j p) n -> p j n", p=128),
                              writes=[HSb_[g_]])
                if samp:
                    ssd_loads(0)
                for seq in range(nseq):
                    Sb, HSb = (Salt, HSalt) if (samp and seq % 2 == 1) else (Sst, HSg)
                    if samp:
                        if seq + 1 < nseq:
                            ssd_loads(seq + 1)
                        tt("dve", CmT[:, :, :], BCT[:, 2:4, off:off + L], blkm[:, seq, :].unsqueeze(1).to_broadcast([128, 2, 128]), ALU.mult,
                           [HBCT, HC], [HCMT])
                        ts("dve", Bm[0:L, :, :].rearrange("p g t -> p (g t)"), B_tok[0:L, si, :], sel2[0:L, seq:seq + 1], ALU.mult, [HBT, HC], [HBM])
                    for g in range(2):
                        for q in range(2):
                            if sonly:
                                break
                            for jj in range(4):
                                j = g * 8 + q * 4 + jj
                                tr(PB[6][:, jj * 128:(jj + 1) * 128], Sb[:, j, :], identf[:, :], [HSb[g], HC], [HPB[6]])
                            c0 = (g * 8 + q * 4) * 128
                            cp("act", S0T[:, c0:c0 + 512], PB[6][:, :], [HPB[6]], [HS0T])
                        lhsC = CmT[:, g, 0:L] if samp else BCT[:, 2 + g, off:off + L]
                        hlc = HCMT if samp else HBCT
                        for b2 in range(2):
                            if sonly:
                                break
                            pyi, hpyi = PYI[g * 2 + b2]
                            mm(pyi[0:L, :], lhsC, S0T[:, g * 1024 + b2 * 512:g * 1024 + b2 * 512 + 512], seq == 0, seq == nseq - 1, [hlc, HS0T], [hpyi])
                        rhsB = Bm[0:L, g, :] if samp else B_tok[0:L, si, g * 128:(g + 1) * 128]
                        hrb = HBM if samp else HBT
                        for jj in range(8):
                            j = g * 8 + jj
                            mm(PB[jj // 4][:, (jj % 4) * 128:(jj % 4) * 128 + 128], xtail[0:L, j * 128:(j + 1) * 128], rhsB, True, True, [HXTL, hrb], [HPB[jj // 4]])
                        tt("dve", Sb[:, g * 8:g * 8 + 8, :], Sb[:, g * 8:g * 8 + 8, :], decS[:, g * 8:g * 8 + 8, seq:seq + 1].to_broadcast([128, 8, 128]),
                           ALU.mult, [HSb[g], HdecS], [HSb[g]])
                        for q in range(2):
                            tt("dve", Sb[:, g * 8 + q * 4:g * 8 + q * 4 + 4, :], Sb[:, g * 8 + q * 4:g * 8 + q * 4 + 4, :],
                               PB[q][:, :].rearrange("p (j n) -> p j n", j=4), ALU.add, [HSb[g], HPB[q]], [HSb[g]])
                        if samp:
                            S.dma("sp", s_ssd[seq, g * 1024:(g + 1) * 1024, :].rearrange("(j p) n -> p j n", p=128), Sb[:, g * 8:g * 8 + 8, :],
                                  reads=[HSb[g]], is_out=True)
                for g in range(2):
                    for b2 in range(2):
                        if sonly:
                            break
                        pyi, hpyi = PYI[g * 2 + b2]
                        cols = slice(g * 1024 + b2 * 512, g * 1024 + b2 * 512 + 512)
                        h0 = g * 16 + b2 * 8
                        et = etmp[ectr[0] % 2]; het = Hetmp[ectr[0] % 2]; ectr[0] += 1
                        tt("dve", et[0:L, :].rearrange("p (h q) -> p h q", q=64), pyi[0:L, :].rearrange("p (h q) -> p h q", q=64),
                           ecum[:, h0:h0 + 8].unsqueeze(2).to_broadcast([L, 8, 64]), ALU.mult, [hpyi, hm], [het])
                        tt("dve", BIG[si][0:L, cols], BIG[si][0:L, cols], et[0:L, :], ALU.add, [HBIG[si], het], [HBIG[si]])
                if samp:
                    S.alias(HSalt, [HWSL[wsl_i]])
                if s_.get("last"):
                    S.dma("pool", p_ssd.rearrange("(j p) n -> p j n", p=128), Sst[:], reads=HSg, is_out=True)
                dbgstage("yssd", si, s_)

            if not sonly:
                for wb in range(4):
                    wt, hw = wload("in", Z0 + wb * 512, 512, 0, KC)
                    for si, s_ in enumerate(tile):
                        L = s_["cfg"].L; off = offs[si]
                        pz = PB[(wb % 2) * 3 + si]; hpz = HPB[(wb % 2) * 3 + si]
                        for kc in range(KC):
                            mm(pz[0:L, :], xnT[:, kc, off:off + L], wt[:, kc, :], kc == 0, kc == KC - 1, [hw, HXN], [hpz])
                        et = etmp[ectr[0] % 2]; het = Hetmp[ectr[0] % 2]; ectr[0] += 1
                        act(et[0:L, :], pz[0:L, :], AF.Silu, [hpz], [het])
                        tt("dve", BIG[si][0:L, wb * 512:(wb + 1) * 512], BIG[si][0:L, wb * 512:(wb + 1) * 512], et[0:L, :], ALU.mult,
                           [HBIG[si], het], [HBIG[si]])
                for si, s_ in enumerate(tile):
                    norm_T(si, s_["cfg"].L, offs[si], snw, ygT[:, 0:16, :], HYG)

            qT_ap, HQT = AR.view("qT", base, 8 * TMAX)
            qT = qT_ap.rearrange("p (h t) -> p h t", h=8)
            kT_ap, HKT = AR.view("kT", base + 3072, 8 * TMAX)
            kT = kT_ap.rearrange("p (h t) -> p h t", h=8)
            ktok_ap, HKTOK = AR.view("k_tok", base + 6144, 3 * 1024)
            k_tok = ktok_ap.rearrange("p (s c) -> p s c", s=3)
            vtok_ap, HVT = AR.view("v_tok", base + 9216, 3 * 2048)
            v_tok = vtok_ap.rearrange("p (s c) -> p s c", s=3)
            wt_ap, HWT = AR.view("WT", base + 15360, 1024)
            WT = wt_ap.rearrange("p (h t) -> p h t", h=8)
            qm_ap, HQM = AR.view("qm", base + 16384, 1024)
            qm = qm_ap.rearrange("p (h t) -> p h t", h=8)
            ks, HKS = AR.view("ks", base + 17408, 1024)
            csc_ap0, HCsc0 = AR.view("Csc0", base + 18432, 1024)
            csc_ap1, HCsc1 = AR.view("Csc1", base + 19456, 1024)
            HCsch = [HCsc0, HCsc1]
            Csc = arena_t[:, base + 18432:base + 20480].rearrange("p (h v) -> p h v", h=8)
            ksall, HKSA = AR.view("ksall", base + 20480, 1024)
            for which, c00, dstT_, hdst_ in ((0, Q0, qT, HQT), (1, K0, kT, HKT)):
                if sonly and which == 0:
                    continue
                for wb in range(2):
                    wt, hw = wload("in", c00 + wb * 512, 512, 0, KC)
                    for sbk in range(4):
                        h = wb * 4 + sbk
                        pacc = PB[3 + h % 3]; hpacc = HPB[3 + h % 3]
                        for kc in range(KC):
                            mm(pacc[:, 0:T], wt[:, kc, sbk * 128:(sbk + 1) * 128], xnT[:, kc, 0:T], kc == 0, kc == KC - 1, [hw, HXN], [hpacc])
                        cp("act", dstT_[:, h, 0:T], pacc[:, 0:T], [hpacc], [hdst_])
            for si, s_ in enumerate(tile):
                L = s_["cfg"].L; off = offs[si]
                for h in range(8):
                    tr(TB[0:L, h * 128:(h + 1) * 128], kT[:, h, off:off + L], identb[:, :], [HKT, HC], [HTB])
                cp("dve", k_tok[0:L, si, :], TB[0:L, :], [HTB], [HKTOK])
            for wb in range(4):
                wt, hw = wload("in", V0 + wb * 512, 512, 0, KC)
                for si, s_ in enumerate(tile):
                    L = s_["cfg"].L; off = offs[si]
                    pv = PB[(wb % 2) * 3 + si]; hpv = HPB[(wb % 2) * 3 + si]
                    for kc in range(KC):
                        mm(pv[0:L, :], xnT[:, kc, off:off + L], wt[:, kc, :], kc == 0, kc == KC - 1, [hw, HXN], [hpv])
                    cp("act", v_tok[0:L, si, wb * 512:(wb + 1) * 512], pv[0:L, :], [hpv], [HVT])
            wt, hw = wload("in", IF0, 16, 0, KC)
            for si, s_ in enumerate(tile):
                L = s_["cfg"].L; off = offs[si]
                for kc in range(KC):
                    mm(PB[si][0:L, 0:16], xnT[:, kc, off:off + L], wt[:, kc, 0:16], kc == 0, kc == KC - 1, [hw, HXN], [HPB[si]])
                tt("dve", sms[si][0:L, 288:296], PB[si][0:L, 0:8], ibb[0:L, :], ALU.add, [HPB[si], HP], [Hsms[si]])
                tt("dve", sms[si][0:L, 296:304], PB[si][0:L, 8:16], fbb[0:L, :], ALU.add, [HPB[si], HP], [Hsms[si]])

            for si, s_ in enumerate(tile):
                c = s_["cfg"]; L = c.L; nseq = c.nseq; Lseq = c.Lseq; off = offs[si]
                samp = s_["kind"] == "samp"
                m_ = sms[si]; hm = Hsms[si]
                ig = m_[0:L, 288:296]; fx = m_[0:L, 296:304]; nf = m_[0:L, 304:312]; l2 = m_[0:L, 312:320]; lf = m_[0:L, 320:328]
                g_ = m_[0:L, 328:336]; sk = m_[0:L, 336:344]; thr = m_[0:L, 344:352]; sks = m_[0:L, 352:360]
                aden = m_[0:L, 360:368]; rd = m_[0:L, 368:376]; ssh = m_[0:L, 376:384]
                dn = m_[0:L, 0:8]; sksm = m_[0:L, 8:16]
                nsf = sm[:, 8:16]
                stt("dve", nf, fx, -1.0, fx, ALU.mult, ALU.min, [hm], [hm])
                act(nf, nf, AF.Exp, [hm], [hm])
                act(l2, nf, AF.Ln, [hm], [hm], bias=1.0)
                stt("dve", lf, fx, 0.0, l2, ALU.min, ALU.subtract, [hm], [hm])
                mm(PB[0][0:L, 0:8], c.U, lf, True, True, [hm, HC], [HPB[0]])
                tt("dve", g_, ig, PB[0][0:L, 0:8], ALU.subtract, [hm, HPB[0]], [hm])
                mm(PB[1][0:8, 0:L], g_, identf[0:L, 0:L], True, True, [hm, HC], [HPB[1]])
                mm(PB[1][0:8, 128:128 + L], lf, c.U, True, True, [hm, HC], [HPB[1]])
                mm(PB[1][0:8, 256:256 + nseq], lf, c.sel, True, True, [hm, HC], [HPB[1]])
                if samp:
                    m0T = mall[:, 0:16]; mnT = mnewT[:, 0:16]; hm0 = Hmall
                else:
                    m0T = mstT[:, 0:1]; mnT = mstT[:, 0:1]; hm0 = Hm
                cmax = smT[0:8, 0:nseq]; cc = smT[0:8, 16:16 + nseq]; decT = smT[0:8, 32:32 + nseq]
                skT = smT[0:8, 128:128 + L]; thrT = smT[0:8, 256:256 + L]; tmp8 = smT[0:8, 384:384 + L]
                decexp = smT[0:8, 512:512 + 8 * nseq]
                S.op("dve", (lambda o_, i_: (lambda e: e.tensor_reduce(out=o_, in_=i_, axis=AX.X, op=ALU.max)))(
                    cmax, PB[1][0:8, 0:L].rearrange("p (s t) -> p s t", t=Lseq)), reads=[HPB[1]], writes=[HsmT])
                tt("dve", cc, cmax, m0T, ALU.max, [HsmT, hm0], [HsmT])
                tt("dve", decT, m0T, cc, ALU.subtract, [HsmT, hm0], [HsmT])
                act(decT, decT, AF.Exp, [HsmT], [HsmT])
                tt("dve", mnT, PB[1][0:8, 256:256 + nseq], cc, ALU.add, [HPB[1], HsmT], [hm0])
                tt("dve", tmp8.rearrange("p (s t) -> p s t", t=Lseq), PB[1][0:8, 0:L].rearrange("p (s t) -> p s t", t=Lseq),
                   cc.unsqueeze(2).to_broadcast([8, nseq, Lseq]), ALU.subtract, [HPB[1], HsmT], [HsmT])
                act(skT, tmp8, AF.Exp, [HsmT], [HsmT])
                stt("dve", tmp8.rearrange("p (s t) -> p s t", t=Lseq), PB[1][0:8, 128:128 + L].rearrange("p (s t) -> p s t", t=Lseq), -1.0,
                    cc.unsqueeze(2).to_broadcast([8, nseq, Lseq]), ALU.mult, ALU.subtract, [HPB[1], HsmT], [HsmT])
                act(thrT, tmp8, AF.Exp, [HsmT], [HsmT])
                mm(PB[0][0:L, 8:16], skT, identf[0:8, 0:8], True, True, [HsmT, HC], [HPB[0]])
                mm(PB[0][0:L, 16:24], thrT, identf[0:8, 0:8], True, True, [HsmT, HC], [HPB[0]])
                cp("dve", sk, PB[0][0:L, 8:16], [HPB[0]], [hm])
                cp("dve", thr, PB[0][0:L, 16:24], [HPB[0]], [hm])
                ts("dve", sks, sk, 128.0 ** -0.5, ALU.mult, [hm], [hm])
                tt("dve", decexp.rearrange("p (h s) -> p h s", s=nseq), decT.unsqueeze(1).to_broadcast([8, 8, nseq]),
                   identf[0:8, 0:8].unsqueeze(2).to_broadcast([8, 8, nseq]), ALU.mult, [HsmT, HC], [HsmT])
                mm(PB[0][:, 32:32 + 8 * nseq], onesf[0:8, :], decexp, True, True, [HsmT, HC], [HPB[0]])
                cp("dve", decbc[:, :, 0:nseq], PB[0][:, 32:32 + 8 * nseq].rearrange("p (h s) -> p h s", s=nseq), [HPB[0]], [Hdecbc])
                if not sonly:
                    for hg in range(2):
                        pq = (PB[1], PB[6])[hg]; hpq = (HPB[1], HPB[6])[hg]
                        for j in range(4):
                            h = hg * 4 + j
                            mm(pq[0:L, j * 128:j * 128 + L], kT[:, h, off:off + L], qT[:, h, off:off + L], True, True, [HKT, HQT], [hpq])
                        for j in range(4):
                            h = hg * 4 + j
                            stt("dve", WT[0:L, h, 0:L], pq[0:L, j * 128:j * 128 + L], sks[:, h:h + 1], c.U, ALU.mult, ALU.mult, [hpq, hm, HC], [HWT])

                def nump(h):
                    return PB[2 + h // 2][0:L, (h % 2) * 256:(h % 2) * 256 + 256], HPB[2 + h // 2]
                tt("dve", ksall[0:L, :].rearrange("p (h d) -> p h d", h=8), k_tok[0:L, si, :].rearrange("p (h d) -> p h d", h=8),
                   sks.unsqueeze(2).to_broadcast([L, 8, 128]), ALU.mult, [HKTOK, hm], [HKSA])
                selb = sel2b[0:L, 0:16] if samp else onesb[0:L, 0:1]
                for h in range(8):
                    mm(PB[0][:, 256 + h * nseq:256 + (h + 1) * nseq], ksall[0:L, h * 128:(h + 1) * 128], selb, True, True, [HKSA, HC], [HPB[0]])
                cp("dve", nupd[:, :, 0:nseq], PB[0][:, 256:256 + 8 * nseq].rearrange("p (h s) -> p h s", s=nseq), [HPB[0]], [Hnupd])
                for h in range(8):
                    np_, hnp = nump(h)
                    if debug == "numinter" or sonly:
                        continue
                    mm(np_, WT[0:L, h, 0:L], v_tok[0:L, si, h * 256:(h + 1) * 256], h % 2 == 0, False, [HWT, HVT], [hnp])
                    mm(PB[0][0:L, 200 + h:201 + h], WT[0:L, h, 0:L], onesb[0:L, 0:1], h == 0, False, [HWT, HC], [HPB[0]])
                if samp:
                    wsl_i = wctr[0] % 2
                    HCalt = [H("Calt0"), H("Calt1")]
                    S.alias([HWSL[wsl_i]], HCalt)
                    Calt = WSL[wsl_i][:].rearrange("p k c -> p (k c)")[:, 0:4096].bitcast(F32).rearrange("p (h v) -> p h v", h=8)
                def ml_loads(sq):
                    Cb_, HCb_ = (Calt, HCalt) if sq % 2 == 1 else (Cst, HCh)
                    for hf_ in range(2):
                        S.dma("sp", Cb_[:, hf_ * 4:hf_ * 4 + 4, :], st_C[sq, hf_ * 4:hf_ * 4 + 4].rearrange("h d v -> d h v"), writes=[HCb_[hf_]])
                if samp:
                    ml_loads(0)
                for seq in range(nseq):
                    Cb, HCb = (Calt, HCalt) if (samp and seq % 2 == 1) else (Cst, HCh)
                    if samp:
                        if seq + 1 < nseq:
                            ml_loads(seq + 1)
                        n0 = nall[:, seq, :]; nnew_ap = nnew[:, seq, :]; hn0 = Hnall
                        tt("dve", qm[:, :, :], qT[:, :, off:off + L], blkm[:, seq, :].unsqueeze(1).to_broadcast([128, 8, 128]), ALU.mult,
                           [HQT, HC], [HQM])
                        ts("dve", ks[0:L, :], ksall[0:L, :], sel2[0:L, seq:seq + 1], ALU.mult, [HKSA, HC], [HKS])
                        ks_use = ks; hks_use = HKS
                    else:
                        n0 = nst[:, :]; nnew_ap = nst[:, :]; hn0 = Hn
                        ks_use = ksall; hks_use = HKSA
                    tt("dve", nsf, n0, decbc[:, :, seq], ALU.mult, [hn0, Hdecbc], [Hsm])
                    if not sonly:
                        cp("dve", nsc[:, :], nsf, [Hsm], [Hnsc])
                    for half in range(2):
                        hs_ = slice(half * 4, half * 4 + 4)
                        tt("dve", Cb[:, hs_, :], Cb[:, hs_, :], decbc[:, hs_, seq:seq + 1].to_broadcast([128, 4, 256]), ALU.mult,
                           [HCb[half], Hdecbc], [HCb[half]])
                        if not sonly:
                            cp("act", Csc[:, hs_, :], Cb[:, hs_, :], [HCb[half]], [HCsch[half]])
                        for h in range(half * 4, half * 4 + 4):
                            if sonly:
                                break
                            np_, hnp = nump(h)
                            lq = qm[:, h, 0:L] if samp else qT[:, h, off:off + L]
                            hlq = HQM if samp else HQT
                            st_flag = (debug == "numinter" and seq == 0)
                            mm(np_, lq, Csc[:, h, :], st_flag and h % 2 == 0, seq == nseq - 1 and h % 2 == 1, [hlq, HCsch[half]], [hnp])
                            mm(PB[0][0:L, 200 + h:201 + h], lq, nsc[:, h:h + 1], st_flag and h == 0, seq == nseq - 1 and h == 7, [hlq, Hnsc], [HPB[0]])
                        for jj in range(4):
                            h = half * 4 + jj
                            pc = (PB[1], PB[6])[jj // 2]; hpc = (HPB[1], HPB[6])[jj // 2]
                            mm(pc[:, (jj % 2) * 256:(jj % 2) * 256 + 256], ks_use[0:L, h * 128:(h + 1) * 128], v_tok[0:L, si, h * 256:(h + 1) * 256],
                               True, True, [hks_use, HVT], [hpc])
                        for q in range(2):
                            pc = (PB[1], PB[6])[q]; hpc = (HPB[1], HPB[6])[q]
                            h0 = half * 4 + q * 2
                            tt("dve", Cb[:, h0:h0 + 2, :], Cb[:, h0:h0 + 2, :], pc[:, :].rearrange("p (h v) -> p h v", h=2), ALU.add,
                               [HCb[half], hpc], [HCb[half]])
                        if samp:
                            S.dma("sp", s_C[seq, hs_].rearrange("h d v -> d h v"), Cb[:, hs_, :], reads=[HCb[half]], is_out=True)
                    tt("dve", nnew_ap, nsf, nupd[:, :, seq], ALU.add, [Hsm, Hnupd], [hn0])
                if samp:
                    S.alias(HCalt, [HWSL[wsl_i]])
                if not sonly:
                    cp("dve", dn, PB[0][0:L, 200:208], [HPB[0]], [hm])
                    stt("dve", aden, dn, -1.0, dn, ALU.mult, ALU.max, [hm], [hm])
                    tt("dve", aden, aden, thr, ALU.max, [hm], [hm])
                    S.op("dve", (lambda o_, i_: (lambda e: e.reciprocal(out=o_, in_=i_)))(rd, aden), reads=[hm], writes=[hm])
                    if debug == "numinter":
                        S.op("dve", (lambda o_: (lambda e: e.memset(o_, 1.0)))(rd), reads=[hm], writes=[hm])
                    for bk in range(4):
                        tt("dve", BIG[si][0:L, bk * 512:(bk + 1) * 512].rearrange("p (h v) -> p h v", h=2),
                           PB[2 + bk][0:L, :].rearrange("p (h v) -> p h v", h=2), rd[:, 2 * bk:2 * bk + 2].unsqueeze(2).to_broadcast([L, 2, 256]),
                           ALU.mult, [HPB[2 + bk], hm], [HBIG[si]])
                    dbgstage("hml", si, s_)
                    dbgstage("numinter", si, s_)
                    S.op("dve", (lambda o_: (lambda e: e.memset(o_, 0.0)))(ssh), reads=[hm], writes=[hm])
                    junk, Hjunk = AR.view("junk", JUNK0, 2048)
                    for h in range(8):
                        act(junk[0:L, 0:256], BIG[si][0:L, h * 256:(h + 1) * 256], AF.Square, [HBIG[si], hm], [Hjunk, hm], accum=ssh[:, h:h + 1])
                    rsqrt_mean(ssh, ssh, 256, L, [hm])
                    tt("dve", BIG[si][0:L, :].rearrange("p (h v) -> p h v", h=8), BIG[si][0:L, :].rearrange("p (h v) -> p h v", h=8),
                       ssh.unsqueeze(2).to_broadcast([L, 8, 256]), ALU.mult, [HBIG[si], hm], [HBIG[si]])
                if s_.get("last"):
                    S.dma("pool", p_C.rearrange("h d v -> d h v"), Cst[:], reads=HCh, is_out=True)
                    mm(PB[0][0:8, 0:128], nst[:, :], identf[:, :], True, True, [Hn, HC], [HPB[0]])
                    cp("dve", smT[0:8, 0:128], PB[0][0:8, 0:128], [HPB[0]], [HsmT])
                    S.dma("pool", p_n[:, :], smT[0:8, 0:128], reads=[HsmT], is_out=True)
                    S.dma("pool", p_m.rearrange("o h -> h o"), mstT[:, 0:1], reads=[Hm], is_out=True)
                if samp:
                    mm(PB[0][:, 0:128], nnew[:].rearrange("p s h -> p (s h)"), identf[:, :], True, True, [Hnall, HC], [HPB[0]])
                    cp("dve", etmp[0][:, 0:128], PB[0][:, 0:128], [HPB[0]], [Hetmp[0]])
                    S.dma("pool", s_n[:, :], etmp[0][:, 0:128], reads=[Hetmp[0]], is_out=True)
                    tr(PB[0][0:16, 0:8], mnewT[:, 0:16], identf[0:8, 0:8], [Hmall, HC], [HPB[0]])
                    cp("dve", pstage[0:16, 0:8], PB[0][0:16, 0:8], [HPB[0]], [Hpst])
                    S.dma("pool", s_m[:, :], pstage[0:16, 0:8], reads=[Hpst], is_out=True)

            if sonly:
                if ti == n_pre_tiles - 1:
                    S.dma("sp", pmk[:, :], pmask_d.partition_broadcast(128), writes=[Hpmk])
                    ts("dve", Sst[:].rearrange("p j n -> p (j n)"), Sst[:].rearrange("p j n -> p (j n)"), pmk[:, 0:1], ALU.mult, HSg + [Hpmk], HSg)
                    ts("dve", Cst[:].rearrange("p h v -> p (h v)"), Cst[:].rearrange("p h v -> p (h v)"), pmk[:, 0:1], ALU.mult, HCh + [Hpmk], HCh)
                    ts("dve", nst[:, :], nst[:, :], pmk[:, 0:1], ALU.mult, [Hn, Hpmk], [Hn])
                    ts("dve", mstT[:, :], mstT[:, :], pmk[0:8, 0:1], ALU.mult, [Hm, Hpmk], [Hm])
                    ts("dve", halo_p[:].rearrange("p b t -> p (b t)"), halo_p[:].rearrange("p b t -> p (b t)"), pmk[:, 0:1], ALU.mult, [Hhp, Hpmk], [Hhp])
                continue

            for wb in range(4):
                wt, hw = wload("in", O0 + wb * 512, 512, 0, KC)
                for si, s_ in enumerate(tile):
                    L = s_["cfg"].L; off = offs[si]
                    po = PB[(wb % 2) * 3 + si]; hpo = HPB[(wb % 2) * 3 + si]
                    for kc in range(KC):
                        mm(po[0:L, :], xnT[:, kc, off:off + L], wt[:, kc, :], kc == 0, kc == KC - 1, [hw, HXN], [hpo])
                    et = etmp[ectr[0] % 2]; het = Hetmp[ectr[0] % 2]; ectr[0] += 1
                    act(et[0:L, :], po[0:L, :], AF.Sigmoid, [hpo], [het])
                    tt("dve", BIG[si][0:L, wb * 512:(wb + 1) * 512], BIG[si][0:L, wb * 512:(wb + 1) * 512], et[0:L, :], ALU.mult,
                       [HBIG[si], het], [HBIG[si]])
            for si, s_ in enumerate(tile):
                norm_T(si, s_["cfg"].L, offs[si], mnw, ygT[:, 16:32, :], HYG, prescaled=True)

            hctr = 0
            for cb_i in range(4):
                wA, hwA = wload("out", cb_i * 512, 512, 0, 16)
                wB, hwB = wload("out", cb_i * 512, 512, 16, 16)
                for si, s_ in enumerate(tile):
                    L = s_["cfg"].L; off = offs[si]
                    hr = hres[0]; hhr = Hhres[0]
                    S.dma("sp", hr[0:L, :], s_["src"][:, cb_i * 512:(cb_i + 1) * 512], writes=[hhr])
                    po = PB[(cb_i % 2) * 3 + si]; hpo = HPB[(cb_i % 2) * 3 + si]
                    for kc in range(32):
                        w_, hw_ = (wA, hwA) if kc < 16 else (wB, hwB)
                        mm(po[0:L, :], ygT[:, kc, off:off + L], w_[:, kc % 16, :], kc == 0, kc == 31, [hw_, HYG], [hpo])
                    tt("dve", BIG[si][0:L, cb_i * 512:(cb_i + 1) * 512], po[0:L, :], hr[0:L, :], ALU.add, [hpo, hhr], [HBIG[si]])
            for si, s_ in enumerate(tile):
                dbgstage("hmid", si, s_)
            xsc_ap, Hxsc = AR.view("xsc", base, 4096)
            xsc = xsc_ap.bitcast(F32)
            for si, s_ in enumerate(tile):
                norm_T(si, s_["cfg"].L, offs[si], n2w, xnT, HXN, scaled_dst=(xsc, Hxsc))

            gT_ap, HGT = AR.view("gT", 0, 44 * TMAX)
            gT = gT_ap.rearrange("p (k t) -> p k t", k=44)
            fnw_ap, Hfnw = AR.view("fnw", 22528, 4096)
            fnw = fnw_ap.bitcast(F32)
            S.dma("sp", fnw[:, :], final_norm_w.partition_broadcast(128), writes=[Hfnw])
            if last_tile:
                fhs_ap, Hfs = AR.view("fhalo_s", 16896, 5632)
                fhalo_s = fhs_ap.bitcast(F32).rearrange("p (b t) -> p b t", b=88)
                for q in range(22):
                    et = etmp[q % 2]; het = Hetmp[q % 2]
                    S.dma("sp", et[0:32, :], st_ffn[:, q * 512:(q + 1) * 512], writes=[het])
                    for j in range(4):
                        tr(PB[2][:, j * 128:j * 128 + 32], et[0:32, j * 128:(j + 1) * 128], identf[0:32, 0:32], [het, HC], [HPB[2]])
                    cp("dve", fhalo_s[:, 4 * q:4 * q + 4, :], PB[2][:].rearrange("p (j l) -> p j l", j=4)[:, :, 0:32], [HPB[2]], [Hfs])
                offp = offs[1]; offS = offs[2]
                cp("dve", xsel[:, :, 0:2], xnT[:, :, offp + 126:offp + 128], [HXN], [Hxsel])
                for kc in range(KC):
                    cp("dve", xsel[:, kc, 2:34].rearrange("p (s t) -> p s t", t=2),
                       xnT[:, kc, offS:offS + 128].rearrange("p (s t) -> p s t", t=LS)[:, :, 6:8], [HXN], [Hxsel])
            else:
                fhalo_s = None; Hfs = None
            for jb in range(22):
                wgv, hwgv = wload("up", jb * 256, 512, 0, KC)
                for sbk in range(2):
                    j = jb * 2 + sbk
                    pg = PB[2 * (j % 3)]; hpg = HPB[2 * (j % 3)]
                    pv = PB[2 * (j % 3) + 1]; hpv = HPB[2 * (j % 3) + 1]
                    for kc in range(KC):
                        mm(pg[:, 0:T], wgv[:, kc, sbk * 128:(sbk + 1) * 128], xnT[:, kc, 0:T], kc == 0, kc == KC - 1, [hwgv, HXN], [hpg])
                    for kc in range(KC):
                        mm(pv[:, 0:T], wgv[:, kc, 256 + sbk * 128:256 + (sbk + 1) * 128], xnT[:, kc, 0:T], kc == 0, kc == KC - 1, [hwgv, HXN], [hpv])
                    gc, hgc = conv_block(pg, hpg, 3, fw[:, j, :], fbi[:, j:j + 1], fhalo_p[:, j, :], Hfp,
                                         fhalo_s[:, j, :] if last_tile else None, Hfs, 0, T, Tp, has_samp, not last_tile)
                    vc, hvc = conv_block(pv, hpv, 3, fw[:, 44 + j, :], fbi[:, 44 + j:45 + j], fhalo_p[:, 44 + j, :], Hfp,
                                         fhalo_s[:, 44 + j, :] if last_tile else None, Hfs, 1, T, Tp, has_samp, not last_tile)
                    act(gc[:, 0:T], gc[:, 0:T], AF.Silu, [hgc], [hgc])
                    tt("dve", gT[:, j, 0:T], gc[:, 0:T], vc[:, 0:T], ALU.mult, [hgc, hvc], [HGT])
                if last_tile:
                    for kc in range(KC):
                        mm(PB[6][0:34, :], xsel[:, kc, 0:34], wgv[:, kc, :], kc == 0, kc == KC - 1, [hwgv, Hxsel], [HPB[6]])
                    cp("act", crow[0:34, :], PB[6][0:34, :], [HPB[6]], [Hcrow])
                    for half, c0 in ((0, jb * 256), (1, DFF + jb * 256)):
                        S.dma("pool", p_ffn[:, c0:c0 + 256], crow[0:2, half * 256:half * 256 + 256], reads=[Hcrow], is_out=True)
                        S.dma("pool", s_ffn[:, c0:c0 + 256], crow[2:34, half * 256:half * 256 + 256], reads=[Hcrow], is_out=True)

            for cb_i in range(4):
                for kg in range(4):
                    wt, hw = wload("dn", cb_i * 512, 512, kg * 11, 11)
                    for si, s_ in enumerate(tile):
                        L = s_["cfg"].L; off = offs[si]
                        pd = PB[(cb_i % 2) * 3 + si]; hpd = HPB[(cb_i % 2) * 3 + si]
                        for kk in range(11):
                            kc = kg * 11 + kk
                            mm(pd[0:L, :], gT[:, kc, off:off + L], wt[:, kk, :], kc == 0, kc == 43, [hw, HGT], [hpd])
                for si, s_ in enumerate(tile):
                    L = s_["cfg"].L
                    pd = PB[(cb_i % 2) * 3 + si]; hpd = HPB[(cb_i % 2) * 3 + si]
                    tt("dve", BIG[si][0:L, cb_i * 512:(cb_i + 1) * 512], BIG[si][0:L, cb_i * 512:(cb_i + 1) * 512], pd[0:L, :], ALU.add,
                       [HBIG[si], hpd], [HBIG[si]])
            for si, s_ in enumerate(tile):
                L = s_["cfg"].L
                if s_["dst"] is None:
                    continue
                S.op("dve", (lambda o_: (lambda e: e.memset(o_, 0.0)))(sm[0:L, 2:3]), writes=[Hsm])
                junk, Hjunk = AR.view("junk", JUNK0, 2048)
                act(junk[0:L, :], BIG[si][0:L, :], AF.Square, [HBIG[si], Hsm], [Hjunk, Hsm], accum=sm[0:L, 2:3])
                rsqrt_mean(sm[0:L, 3:4], sm[0:L, 2:3], D, L, [Hsm])
                stt("dve", BIG[si][0:L, :], BIG[si][0:L, :], sm[0:L, 3:4], fnw[0:L, :], ALU.mult, ALU.mult, [HBIG[si], Hsm, Hfnw], [HBIG[si]])
                if not debug:
                    S.dma("pool", s_["dst"], BIG[si][0:L, :], reads=[HBIG[si]], is_out=True)

        S.finish()
        S.emit_all()
    return nc


_NC_CACHE = {}
PRE_LS = [128] * 7 + [126]
FULL_LS = [18] + [128] * 8


def _get_nc(pre_Ls, full_Ls):
    key = (tuple(pre_Ls), tuple(full_Ls))
    if key not in _NC_CACHE:
        _NC_CACHE[key] = build_nc(list(pre_Ls), list(full_Ls))
    return _NC_CACHE[key]


def make_in_map(h2, G, x_sample_c, st, w, npre, nfull):
    f = lambda a: np.ascontiguousarray(a, dtype=np.float32)
    xfull = G[0:nfull] if h2 == 0 else G[npre:npre + nfull]
    m = {
        "xpre": f(G[0:npre]), "xfull": f(xfull), "xs": f(x_sample_c.reshape(128, D)),
        "pmask": np.array([float(h2)], dtype=np.float32),
        "st_conv": f(st["state_ssd_conv"].reshape(48, 2560)), "st_ssd": f(st["state_ssd"].reshape(16, 2048, 128)),
        "st_C": f(st["state_mlstm_C"]), "st_n": f(st["state_mlstm_n"].reshape(128, 128)), "st_m": f(st["state_mlstm_m"]),
        "st_ffn": f(st["state_ffn_conv"].reshape(32, 2 * DFF)),
        "w_in": f(w["w_in"][0]), "w_out": f(w["w_out"][0]), "w_up": f(w["w_up"][0]), "w_down": f(w["w_down"][0]),
        "norm1_w": f(w["norm1_w"][0]), "norm2_w": f(w["norm2_w"][0]), "ssd_norm_w": f(w["ssd_norm_w"][0]),
        "ml_norm_w": f(w["ml_norm_w"][0]), "final_norm_w": f(w["final_norm_w"]),
        "ssd_conv_w": f(w["ssd_conv_w"][0]), "ssd_conv_b": f(w["ssd_conv_b"][0]), "ssd_dt_bias": f(w["ssd_dt_bias"][0]),
        "ssd_A_log": f(w["ssd_A_log"][0]), "ssd_D": f(w["ssd_D"][0]), "ml_i_bias": f(w["ml_i_bias"][0]),
        "ml_f_bias": f(w["ml_f_bias"][0]), "ffn_conv_w": f(w["ffn_conv_w"][0]), "ffn_conv_b": f(w["ffn_conv_b"][0]),
    }
    return m


def kernel(**inp):
    n = 8
    pre_Ls, full_Ls = PRE_LS, FULL_LS
    npre, nfull = sum(pre_Ls), sum(full_Ls)
    SEQ_ = inp["x_prompt"].shape[1]
    assert npre + nfull == NMETA + SEQ_
    nc = _get_nc(pre_Ls, full_Ls)
    in_maps = []
    for c in range(n):
        b, h2 = c // 2, c % 2
        sl = slice(16 * c, 16 * c + 16)
        st = {k: np.asarray(inp[k])[0, sl] for k in ("state_ssd_conv", "state_ssd", "state_mlstm_C", "state_mlstm_n",
                                                      "state_mlstm_m", "state_ffn_conv")}
        G = np.concatenate([np.asarray(inp["meta_tokens"], dtype=np.float32), np.asarray(inp["x_prompt"])[b]], 0)
        in_maps.append(make_in_map(h2, G, np.asarray(inp["x_sample"])[sl], st, inp, npre, nfull))
    res = run_bass_kernel_spmd(nc, in_maps, core_ids=list(range(n)))
    R = res.results
    half = SEQ_ // 2
    y_prompt = np.stack([np.concatenate([R[2 * b]["yp"][NMETA:NMETA + half], R[2 * b + 1]["yp"][NMETA + half - npre:]], 0)
                         for b in range(4)], 0)
    y_sample = np.concatenate([R[c]["ys"].reshape(16, 8, D) for c in range(n)], 0)
    p = lambda k, shp: np.stack([R[2 * b + 1][k].reshape(shp) for b in range(4)], 0)[None]
    s = lambda k, shp: np.concatenate([R[c][k].reshape((16,) + shp) for c in range(n)], 0)[None]
    outs = (y_prompt, y_sample,
            p("p_conv", (3, 2560)), p("p_ssd", (32, 64, 128)), p("p_C", (8, 128, 256)), p("p_n", (8, 128)), p("p_m", (8,)),
            p("p_ffn", (2, 2 * DFF)),
            s("s_conv", (3, 2560)), s("s_ssd", (32, 64, 128)), s("s_C", (8, 128, 256)), s("s_n", (8, 128)), s("s_m", (8,)),
            s("s_ffn", (2, 2 * DFF)))
    return tuple(np.ascontiguousarray(o, dtype=np.float32) for o in outs)
```

```python
import contextlib
import numpy as np
import concourse.bass as bass
import concourse.mybir as mybir
from concourse.bass_utils import run_bass_kernel_spmd

F32 = mybir.dt.float32
BF16 = mybir.dt.bfloat16
AF = mybir.ActivationFunctionType
ALU = mybir.AluOpType
AX = mybir.AxisListType

D = 2048
KC = 16
NMETA = 16
INC = 10800
Z0, XBC0, DT0, Q0, K0, V0, IF0, O0 = 0, 2048, 4608, 4640, 5664, 6688, 8736, 8752
DFF = 5632
EPS = 1e-6
NSEQ_S = 16
LS = 8

ENGS = ("pe", "act", "dve", "pool", "sp")
NDMA = {'sp': 24, 'pool': 48, 'act': 8}


class H:
    __slots__ = ("name", "w", "r", "excl")

    def __init__(self, name="", excl=False):
        self.name = name
        self.w = {}
        self.r = {}
        self.excl = excl


class Sched:
    def __init__(self, nc):
        self.nc = nc
        self.ops = {e: [] for e in ENGS}
        self.cnt = {e: 0 for e in ENGS}
        self.seen = {e: {} for e in ENGS}
        self.dma_n = {"sp": 0, "pool": 0, "act": 0}
        self.sems = {}
        self.final = []
        self.final_eng = "sp"
        self.outs = {}

    def _deps(self, reads, writes, eng=None):
        d = {}
        for h in reads:
            for k, v in h.w.items():
                if d.get(k, 0) < v:
                    d[k] = v
            if h.excl:
                for k, v in h.r.items():
                    if k != eng and d.get(k, 0) < v:
                        d[k] = v
        for h in writes:
            for k, v in h.w.items():
                if d.get(k, 0) < v:
                    d[k] = v
            for k, v in h.r.items():
                if d.get(k, 0) < v:
                    d[k] = v
        return d

    def _mark(self, reads, writes, key, val):
        for h in reads:
            if h.r.get(key, 0) < val:
                h.r[key] = val
        for h in writes:
            h.w = {key: val}
            h.r = {}

    def op(self, eng, emit, reads=(), writes=()):
        d = self._deps(reads, writes, eng)
        waits = []
        seen = self.seen[eng]
        for k, v in d.items():
            if k == eng and eng == "pe":
                continue
            if seen.get(k, 0) < v:
                seen[k] = v
                waits.append((k, v))
        self.cnt[eng] += 1
        idx = self.cnt[eng]
        self.ops[eng].append((waits, emit, (eng, 1)))
        self._mark(reads, writes, eng, idx)

    def dma(self, q, out, in_, reads=(), writes=(), is_out=False):
        d = self._deps(reads, writes)
        j = self.dma_n[q]
        self.dma_n[q] += 1
        key = ("dma", q, j % NDMA[q])
        val = 16 * (j // NDMA[q] + 1)
        if val > 16:
            d[key] = max(d.get(key, 0), val - 16)
        waits = []
        seen = self.seen[q]
        for k, v in d.items():
            if seen.get(k, 0) < v:
                seen[k] = v
                waits.append((k, v))
        self.ops[q].append((waits, (lambda e: e.dma_start(out=out, in_=in_)), (key, 16)))
        self._mark(reads, writes, key, val)
        if is_out:
            self.outs[key] = max(self.outs.get(key, 0), val)

    def alias(self, olds, news):
        w = {}
        for h in olds:
            for src in (h.w, h.r):
                for k, v in src.items():
                    if w.get(k, 0) < v:
                        w[k] = v
        for h in news:
            for k, v in w.items():
                if h.w.get(k, 0) < v:
                    h.w[k] = v

    def finish(self, eng="sp"):
        self.final = list(self.outs.items())
        self.final_eng = eng

    def emit_all(self):
        nc = self.nc
        keys = list(ENGS)
        for q in ("sp", "pool", "act"):
            if self.dma_n[q]:
                for i in range(NDMA[q]):
                    keys.append(("dma", q, i))
        with contextlib.ExitStack() as es:
            for k in keys:
                nm = k if isinstance(k, str) else "d_%s_%d" % (k[1], k[2])
                self.sems[k] = es.enter_context(nc.semaphore("s_" + nm))
            block = es.enter_context(nc.Block())
            sems = self.sems

            def run(engname):
                def f(e):
                    for waits, emit, inc in self.ops[engname]:
                        for k, v in waits:
                            e.wait_ge(sems[k], v)
                        emit(e).then_inc(sems[inc[0]], inc[1])
                    if self.final and self.final_eng == engname:
                        for k, v in self.final:
                            e.wait_ge(sems[k], v)
                return f

            block.tensor(run("pe"))
            block.scalar(run("act"))
            block.vector(run("dve"))
            block.gpsimd(run("pool"))
            block.sync(run("sp"))


class Arena:
    def __init__(self, S, tensor, size):
        self.S = S
        self.t = tensor
        self.size = size
        self.live = []

    def view(self, name, start, n):
        end = start + n
        assert end <= self.size, (name, end, self.size)
        h = H(name)
        olds = [x[2] for x in self.live if x[0] < end and start < x[1]]
        self.S.alias(olds, [h])
        self.live = [x for x in self.live if not (x[0] < end and start < x[1])]
        self.live.append((start, end, h))
        return self.t[:, start:end], h


def build_nc(pre_Ls, full_Ls, debug=False):
    nc = bass.Bass("TRN2", target_bir_lowering=False)
    S = Sched(nc)
    NPRE = sum(pre_Ls); NFULL = sum(full_Ls)

    def din(name, shape, dt=F32):
        return nc.dram_tensor(name, list(shape), dt, kind="ExternalInput").ap()

    def dout(name, shape):
        return nc.dram_tensor(name, list(shape), F32, kind="ExternalOutput").ap()

    def dscr(name, shape, dt=BF16):
        return nc.dram_tensor(name, list(shape), dt, kind="Internal").ap()

    xpre = din("xpre", [NPRE, D]); xfull = din("xfull", [NFULL, D]); xs = din("xs", [128, D]); pmask_d = din("pmask", [1])
    st_conv = din("st_conv", [48, 2560]); st_ssd = din("st_ssd", [16, 2048, 128])
    st_C = din("st_C", [16, 8, 128, 256]); st_n = din("st_n", [128, 128]); st_m = din("st_m", [16, 8])
    st_ffn = din("st_ffn", [32, 2 * DFF])
    w_in = din("w_in", [D, INC]); w_out = din("w_out", [4096, D]); w_up = din("w_up", [D, 2 * DFF])
    w_down = din("w_down", [DFF, D])
    norm1_w = din("norm1_w", [D]); norm2_w = din("norm2_w", [D]); ssd_norm_w = din("ssd_norm_w", [D])
    ml_norm_w = din("ml_norm_w", [D]); final_norm_w = din("final_norm_w", [D])
    ssd_conv_w = din("ssd_conv_w", [4, 2560]); ssd_conv_b = din("ssd_conv_b", [2560])
    ssd_dt_bias = din("ssd_dt_bias", [32]); ssd_A_log = din("ssd_A_log", [32]); ssd_D = din("ssd_D", [32])
    ml_i_bias = din("ml_i_bias", [8]); ml_f_bias = din("ml_f_bias", [8])
    ffn_conv_w = din("ffn_conv_w", [3, 2 * DFF]); ffn_conv_b = din("ffn_conv_b", [2 * DFF])

    yp = dout("yp", [NFULL, D]); ys = dout("ys", [128, D])
    p_conv = dout("p_conv", [3, 2560]); p_ssd = dout("p_ssd", [2048, 128]); p_C = dout("p_C", [8, 128, 256])
    p_n = dout("p_n", [8, 128]); p_m = dout("p_m", [1, 8]); p_ffn = dout("p_ffn", [2, 2 * DFF])
    s_conv = dout("s_conv", [48, 2560]); s_ssd = dout("s_ssd", [16, 2048, 128]); s_C = dout("s_C", [16, 8, 128, 256])
    s_n = dout("s_n", [128, 128]); s_m = dout("s_m", [16, 8]); s_ffn = dout("s_ffn", [32, 2 * DFF])

    WSCR_N = 128 * (KC * INC + 32 * D + KC * 2 * DFF + 44 * D)
    wscr = dscr("wscr", [WSCR_N])

    es = contextlib.ExitStack()
    with es:
        es.enter_context(nc.allow_non_contiguous_dma(reason="small strided parameter/state layouts"))
        _n = [0]

        def sb(shape, dt=F32, name=None):
            _n[0] += 1
            return es.enter_context(nc.sbuf_tensor(name or ("t%d" % _n[0]), list(shape), dt))

        def ps(shape, dt=F32, name=None):
            _n[0] += 1
            return es.enter_context(nc.psum_tensor(name or ("p%d" % _n[0]), list(shape), dt))

        PB = [ps([128, 512], F32, "pb%d" % i) for i in range(7)]
        HPB = [H("pb%d" % i, excl=True) for i in range(7)]
        TB = ps([128, 1024], BF16, "tb"); HTB = H("tb", excl=True)

        identf = sb([128, 128]); identb = sb([128, 128], BF16); onesf = sb([128, 128]); onesb = sb([128, 1], BF16)
        U0 = sb([128, 128]); U2 = sb([128, 128]); SS2 = sb([128, 128]); sel2 = sb([128, 16])
        negm0 = sb([128, 128], BF16); negm2 = sb([128, 128], BF16)
        blkm = sb([128, 16, 128], BF16); Eexp = sb([32, 2048]); sel2b = sb([128, 16], BF16)
        HC = H("consts")

        def pop_(f):
            S.op("pool", f, reads=[HC], writes=[HC])
        pop_(lambda e: e.memset(identf[:], 0.0))
        pop_(lambda e: e.affine_select(out=identf[:], in_=identf[:], pattern=[[-1, 128]], compare_op=ALU.not_equal,
                                       fill=1.0, base=0, channel_multiplier=1))
        pop_(lambda e: e.memset(onesf[:], 1.0))
        pop_(lambda e: e.memset(onesb[:], 1.0))
        pop_(lambda e: e.memset(U0[:], 1.0))
        pop_(lambda e: e.affine_select(out=U0[:], in_=U0[:], pattern=[[1, 128]], compare_op=ALU.is_ge,
                                       fill=0.0, base=0, channel_multiplier=-1))
        pop_(lambda e: e.memset(sel2[:], 1.0))
        pop_(lambda e: e.affine_select(out=sel2[:], in_=sel2[:], pattern=[[-8, 16]], compare_op=ALU.is_ge,
                                       fill=0.0, base=0, channel_multiplier=1))
        pop_(lambda e: e.affine_select(out=sel2[:], in_=sel2[:], pattern=[[8, 16]], compare_op=ALU.is_ge,
                                       fill=0.0, base=7, channel_multiplier=-1))
        pop_(lambda e: e.memset(blkm[:], 1.0))
        pop_(lambda e: e.affine_select(out=blkm[:].rearrange("p s (b i) -> p s b i", i=8), in_=blkm[:].rearrange("p s (b i) -> p s b i", i=8),
                                       pattern=[[1, 16], [-1, 16], [0, 8]], compare_op=ALU.is_equal,
                                       fill=0.0, base=0, channel_multiplier=0))
        pop_(lambda e: e.memset(Eexp[:], 1.0))
        pop_(lambda e: e.affine_select(out=Eexp[:].rearrange("p (h q) -> p h q", q=64), in_=Eexp[:].rearrange("p (h q) -> p h q", q=64),
                                       pattern=[[-1, 32], [0, 64]], compare_op=ALU.is_equal,
                                       fill=0.0, base=0, channel_multiplier=1))

        def dop(f):
            S.op("dve", f, reads=[HC], writes=[HC])
        dop(lambda e: e.tensor_copy(out=identb[:], in_=identf[:]))
        dop(lambda e: e.tensor_copy(out=sel2b[:], in_=sel2[:]))
        dop(lambda e: e.tensor_copy(out=SS2[:].rearrange("p (b i) -> p b i", i=8), in_=sel2[:].unsqueeze(2).to_broadcast([128, 16, 8])))
        dop(lambda e: e.tensor_tensor(out=U2[:], in0=U0[:], in1=SS2[:], op=ALU.mult))
        dop(lambda e: e.tensor_scalar(out=negm0[:], in0=U0[:], scalar1=1.0, scalar2=30000.0, op0=ALU.subtract, op1=ALU.mult))
        dop(lambda e: e.tensor_scalar(out=negm2[:], in0=U2[:], scalar1=1.0, scalar2=30000.0, op0=ALU.subtract, op1=ALU.mult))

        class Cfg:
            pass
        cfgs = {}

        def get_cfg(name, L):
            key = (name, L)
            if key in cfgs:
                return cfgs[key]
            c = Cfg()
            if name == "samp":
                c.name, c.L, c.nseq, c.Lseq = "samp", 128, 16, 8
                U, SSm, sel, negm = U2, SS2, sel2, negm2
            else:
                c.name, c.L, c.nseq, c.Lseq = "chunk", L, 1, L
                U, SSm, sel, negm = U0, onesf, onesf, negm0
            L_ = c.L
            c.U = U[0:L_, 0:L_]; c.SS = SSm[0:L_, 0:L_]; c.sel = sel[0:L_, 0:c.nseq]
            c.negm = negm[0:L_, 0:L_].unsqueeze(1).to_broadcast([L_, 4, L_])
            c.negm1 = negm[0:L_, 0:L_]
            cfgs[key] = c
            return c

        n1w = sb([128, KC]); n2w = sb([128, KC]); snw = sb([128, KC]); mnw = sb([128, KC])
        dtb = sb([128, 32]); Aneg = sb([128, 32]); Dbc = sb([128, 32]); ibb = sb([128, 8]); fbb = sb([128, 8])
        cw = sb([128, 20, 4]); cbi = sb([128, 20]); fw = sb([128, 88, 3]); fbi = sb([128, 88])
        HP = H("params")
        WSRC = {"in": w_in.rearrange("(kc p) c -> p kc c", p=128),
                "out": w_out.rearrange("(kc p) c -> p kc c", p=128),
                "up": w_up.rearrange("(kc p) c -> p kc c", p=128),
                "dn": w_down.rearrange("(kc p) c -> p kc c", p=128)}
        BLOCKS = []
        for wb in range(5):
            BLOCKS.append(("in", XBC0 + wb * 512, 512, 0, KC))
        BLOCKS.append(("in", DT0, 32, 0, KC))
        for wb in range(4):
            BLOCKS.append(("in", Z0 + wb * 512, 512, 0, KC))
        for c00 in (Q0, K0):
            for wb in range(2):
                BLOCKS.append(("in", c00 + wb * 512, 512, 0, KC))
        for wb in range(4):
            BLOCKS.append(("in", V0 + wb * 512, 512, 0, KC))
        BLOCKS.append(("in", IF0, 16, 0, KC))
        for wb in range(4):
            BLOCKS.append(("in", O0 + wb * 512, 512, 0, KC))
        for cb_i in range(4):
            BLOCKS.append(("out", cb_i * 512, 512, 0, 16))
            BLOCKS.append(("out", cb_i * 512, 512, 16, 16))
        for jb in range(22):
            BLOCKS.append(("up", jb * 256, 512, 0, KC))
        for cb_i in range(4):
            for kg in range(4):
                BLOCKS.append(("dn", cb_i * 512, 512, kg * 11, 11))
        HBLK = {b: H("wblk") for b in BLOCKS}
        BV = {}
        off_ = 0
        for b in BLOCKS:
            n_ = 128 * b[4] * b[2]
            BV[b] = wscr[off_:off_ + n_].rearrange("(p k c) -> p k c", p=128, k=b[4])
            off_ += n_
        assert off_ <= WSCR_N
        cast_ptr = [0]
        cur_ti = [0]
        LOOKAHEAD = 6

        def ensure_cast(upto):
            while cast_ptr[0] < min(upto, len(BLOCKS)):
                b = BLOCKS[cast_ptr[0]]
                cast_ptr[0] += 1
                which, c0, width, k0, nk = b
                src = WSRC[which]
                dst = BV[b]
                for ka in range(0, nk, 8):
                    kb = min(nk, ka + 8)
                    if which == "up":
                        S.dma("pool", dst[:, ka:kb, 0:256], src[:, k0 + ka:k0 + kb, c0:c0 + 256], writes=[HBLK[b]])
                        S.dma("pool", dst[:, ka:kb, 256:512], src[:, k0 + ka:k0 + kb, DFF + c0:DFF + c0 + 256], writes=[HBLK[b]])
                    else:
                        S.dma("pool", dst[:, ka:kb, :], src[:, k0 + ka:k0 + kb, c0:c0 + width], writes=[HBLK[b]])

        pre_subs = []
        o_ = 0
        for L_ in pre_Ls:
            pre_subs.append(dict(cfg=get_cfg("chunk", L_), src=xpre[o_:o_ + L_, :], dst=None, kind="chunk", sonly=True))
            o_ += L_
        full_subs = []
        o_ = 0
        for i, L_ in enumerate(full_Ls):
            full_subs.append(dict(cfg=get_cfg("chunk", L_), src=xfull[o_:o_ + L_, :], dst=yp[o_:o_ + L_, :], kind="chunk",
                                  sonly=False, last=(i == len(full_Ls) - 1)))
            o_ += L_
        samp_sub = dict(cfg=get_cfg("samp", 128), src=xs[:, :], dst=ys[:, :], kind="samp", sonly=False)
        tiles = [pre_subs[i:i + 3] for i in range(0, len(pre_subs), 3)]
        n_pre_tiles = len(tiles)
        rest = full_subs[:-2]
        sizes = {7: (3, 2, 2), 1: (1,), 4: (2, 2), 0: ()}.get(len(rest))
        if sizes is None:
            sizes = tuple([3] * (len(rest) // 3) + ([len(rest) % 3] if len(rest) % 3 else []))
        o_ = 0
        for n_ in sizes:
            tiles.append(rest[o_:o_ + n_])
            o_ += n_
        tiles.append(full_subs[-2:] + [samp_sub])
        TMAX = 384

        xnT = sb([128, KC, TMAX], BF16, "xnT"); HXN = H("xnT")
        BIG = [sb([128, D], F32, "big%d" % i) for i in range(3)]
        HBIG = [H("big%d" % i) for i in range(3)]
        ARN = 36096
        arena_t = sb([128, ARN], BF16, "arena")
        AR = Arena(S, arena_t, ARN)
        WSL = [sb([128, KC, 512], BF16, "wslot%d" % i) for i in range(2)]
        HWSL = [H("wslot%d" % i) for i in range(2)]
        wctr = [0]
        Sst = sb([128, 16, 128], F32, "Sst"); HSg = [H("Sst0"), H("Sst1")]
        Cst = sb([128, 8, 256], F32, "Cst"); HCh = [H("Cst0"), H("Cst1")]
        nst = sb([128, 8], F32, "nst"); nsc = sb([128, 8], BF16, "nsc"); Hn = H("nst"); Hnsc = H("nsc")
        mstT = sb([8, 1], F32, "mstT"); Hm = H("mstT")
        nall = sb([128, 16, 8], F32, "nall"); nnew = sb([128, 16, 8], F32, "nnew"); Hnall = H("nall")
        mall = sb([8, 16], F32, "mall"); mnewT = sb([8, 16], F32, "mnewT"); Hmall = H("mall")
        halo_p = sb([128, 20, 3], F32, "halo_p"); Hhp = H("halo_p")
        halo_s = sb([128, 20, 48], F32, "halo_s"); Hhs = H("halo_s")
        fhalo_p = sb([128, 88, 2], F32, "fhalo_p"); Hfp = H("fhalo_p")
        CBW = 3 + 384 + 16 * 11
        cbuf = [sb([128, CBW], F32, "cbuf%d" % i) for i in range(2)]; Hcbuf = [H("cbuf0"), H("cbuf1")]
        cacc = [sb([128, TMAX], F32, "cacc%d" % i) for i in range(2)]; Hcacc = [H("cacc0"), H("cacc1")]
        etmp = [sb([128, 512], F32, "etmp%d" % i) for i in range(2)]; Hetmp = [H("etmp0"), H("etmp1")]
        ectr = [0]
        hres = [sb([128, 512], F32, "hres%d" % i) for i in range(1)]; Hhres = [H("hres0")]
        crow = sb([64, 512], F32, "crow"); Hcrow = H("crow")
        xsel = sb([128, KC, 64], BF16, "xsel"); Hxsel = H("xsel")
        sm = sb([128, 16], F32, "smalls"); Hsm = H("smalls")
        smT = sb([32, 640], F32, "smallsT"); HsmT = H("smallsT")
        decS = sb([128, 16, 16], F32, "decS"); HdecS = H("decS")
        decbc3 = sb([128, 3, 8, 16], F32, "decbc3"); Hdecbc3 = [H("decbc%d" % i) for i in range(3)]
        CB = sb([128, 2, 128], F32, "CB"); HCB = H("CB")
        nupd = sb([128, 8, 16], F32, "nupd"); Hnupd = H("nupd")
        pmk = sb([128, 1], F32, "pmk"); Hpmk = H("pmk")
        sms = [sb([128, 384], F32, "sms%d" % i) for i in range(3)]; Hsms = [H("sms%d" % i) for i in range(3)]
        AV = {}

        def aview(name, start, n, dt=BF16):
            ap, h = AR.view(name, start, n)
            if dt == F32:
                ap = ap.bitcast(F32)
            AV[name] = (ap, h)
            return ap, h
        JUNK0 = 12288 + 21760

        S.op("pool", lambda e: e.memset(Sst[:], 0.0), writes=HSg)
        S.op("pool", lambda e: e.memset(Cst[:], 0.0), writes=HCh)
        S.op("pool", lambda e: (e.memset(nst[:], 0.0), e.memset(mstT[:], 0.0))[1], writes=[Hn, Hm])
        S.op("pool", lambda e: (e.memset(halo_p[:], 0.0), e.memset(fhalo_p[:], 0.0))[1], writes=[Hhp, Hfp])

        def wload(which, c0, width, k0, nk):
            b = (which, c0, width, k0, nk)
            extra = 3 if 1 <= cur_ti[0] < n_pre_tiles else 0
            ensure_cast(max(BLOCKS.index(b) + 1 + LOOKAHEAD, cast_ptr[0] + extra))
            i = wctr[0] % 2
            wctr[0] += 1
            if width == 512:
                S.dma("sp", WSL[i][:, 0:nk, :], BV[b], reads=[HBLK[b]], writes=[HWSL[i]])
            else:
                S.dma("sp", WSL[i][:, 0:nk, 0:width], BV[b], reads=[HBLK[b]], writes=[HWSL[i]])
            return WSL[i], HWSL[i]

        def wload2_unused(ba, bb):
            ensure_cast(BLOCKS.index(bb) + 1 + LOOKAHEAD)
            i = wctr[0] % 2
            wctr[0] += 1
            for half, b in enumerate((ba, bb)):
                which, c0, width, k0, nk = b
                S.dma("sp", WSL[i][:, 0:nk, half * 256:half * 256 + 256], WSRC[which][0][:, k0:k0 + nk, c0:c0 + width],
                      reads=[HBLK[b]], writes=[HWSL[i]])
            return WSL[i], HWSL[i]

        def dbg(name, ap, h, L):
            return

        def dbgstage(stage, si, s_):
            if debug == stage and s_["dst"] is not None:
                L_ = s_["cfg"].L
                S.dma("pool", s_["dst"], BIG[si][0:L_, :], reads=[HBIG[si]], is_out=True)

        def mm(out, lhsT, rhs, start, stop, reads, writes):
            S.op("pe", lambda e: e.matmul(out, lhsT=lhsT, rhs=rhs, start=start, stop=stop), reads=reads, writes=writes)

        def tr(out, in_, ident, reads, writes):
            S.op("pe", lambda e: e.transpose(out, in_, ident), reads=reads, writes=writes)

        def act(out, in_, func, reads, writes, bias=None, scale=None, accum=None):
            kw = {}
            if bias is not None:
                kw["bias"] = bias
            if scale is not None:
                kw["scale"] = scale
            if accum is not None:
                kw["accum_out"] = accum
            S.op("act", lambda e: e.activation(out=out, in_=in_, func=func, **kw), reads=reads, writes=writes)

        def tt(eng, out, in0, in1, op, reads, writes):
            S.op(eng, lambda e: e.tensor_tensor(out=out, in0=in0, in1=in1, op=op), reads=reads, writes=writes)

        def ts(eng, out, in0, s1, op0, reads, writes, s2=None, op1=None):
            if op1 is None:
                S.op(eng, lambda e: e.tensor_scalar(out=out, in0=in0, scalar1=s1, scalar2=None, op0=op0), reads=reads, writes=writes)
            else:
                S.op(eng, lambda e: e.tensor_scalar(out=out, in0=in0, scalar1=s1, scalar2=s2, op0=op0, op1=op1), reads=reads, writes=writes)

        def stt(eng, out, in0, scalar, in1, op0, op1, reads, writes):
            S.op(eng, lambda e: e.scalar_tensor_tensor(out=out, in0=in0, scalar=scalar, in1=in1, op0=op0, op1=op1),
                 reads=reads, writes=writes)

        def cp(eng, out, in_, reads, writes):
            if eng == "act":
                S.op("act", lambda e: e.activation(out=out, in_=in_, func=AF.Copy), reads=reads, writes=writes)
            else:
                S.op(eng, lambda e: e.tensor_copy(out=out, in_=in_), reads=reads, writes=writes)

        def rsqrt_mean(dst, ss, n, L, reads_writes):
            act(dst, ss, AF.Sqrt, reads_writes, reads_writes, bias=EPS, scale=1.0 / n)
            S.op("dve", lambda e: e.reciprocal(out=dst, in_=dst), reads=reads_writes, writes=reads_writes)

        def norm_T(st, L, off, wT, dstT, hdst, prescaled=False, scaled_dst=None):
            b = BIG[st]; hb = HBIG[st]
            src_ = b; hsrc = hb
            if not prescaled:
                S.op("dve", lambda e: e.memset(sm[0:L, 0:1], 0.0), writes=[Hsm])
                junk, Hjunk = AR.view("junk", JUNK0, 2048)
                act(junk[0:L, :], b[0:L, :], AF.Square, [hb, Hsm], [Hjunk, Hsm], accum=sm[0:L, 0:1])
                rsqrt_mean(sm[0:L, 1:2], sm[0:L, 0:1], D, L, [Hsm])
                if scaled_dst is None:
                    ts("dve", b[0:L, :], b[0:L, :], sm[0:L, 1:2], ALU.mult, [hb, Hsm], [hb])
                else:
                    src_, hsrc = scaled_dst
                    ts("dve", src_[0:L, :], b[0:L, :], sm[0:L, 1:2], ALU.mult, [hb, Hsm], [hsrc])
            for q in range(4):
                pb = PB[q % 2]; hpb = HPB[q % 2]
                for j in range(4):
                    kc = 4 * q + j
                    tr(pb[:, j * 128:j * 128 + L], src_[0:L, kc * 128:(kc + 1) * 128], identf[0:L, 0:L], [hsrc, HC], [hpb])
                tt("dve", dstT[:, 4 * q:4 * q + 4, off:off + L], pb[:].rearrange("p (j l) -> p j l", j=4)[:, :, 0:L],
                   wT[:, 4 * q:4 * q + 4].unsqueeze(2).to_broadcast([128, 4, L]), ALU.mult, [hpb, HP], [hdst])

        pstage = sb([128, 128], F32, "pstage"); Hpst = H("pstage")

        def load_T(dst, src1d, nb):
            S.dma("sp", pstage[0:nb, :], src1d.rearrange("(b p) -> b p", p=128), writes=[Hpst])
            tr(PB[0][:, 0:nb], pstage[0:nb, :], identf[0:nb, 0:nb], [Hpst, HC], [HPB[0]])
            cp("dve", dst, PB[0][:, 0:nb], [HPB[0]], [HP])
        for dst, src in ((n1w, norm1_w), (n2w, norm2_w), (snw, ssd_norm_w), (mnw, ml_norm_w)):
            load_T(dst[:, :], src, KC)
        for j in range(4):
            load_T(cw[:, :, j], ssd_conv_w[j], 20)
        load_T(cbi[:, :], ssd_conv_b, 20)
        for j in range(3):
            load_T(fw[:, :, j], ffn_conv_w[j], 88)
        load_T(fbi[:, :], ffn_conv_b, 88)
        for dst, src in ((dtb, ssd_dt_bias), (Aneg, ssd_A_log), (Dbc, ssd_D), (ibb, ml_i_bias), (fbb, ml_f_bias)):
            S.dma("sp", dst[:], src.partition_broadcast(128), writes=[HP])
        S.op("act", lambda e: e.activation(out=Aneg[:], in_=Aneg[:], func=AF.Exp), reads=[HP], writes=[HP])
        S.op("dve", lambda e: e.tensor_scalar(out=Aneg[:], in0=Aneg[:], scalar1=-1.0, scalar2=None, op0=ALU.mult),
             reads=[HP], writes=[HP])

        SOFF = 3 + 384

        def conv_block(pacc, hpacc, K, w_ap, b_ap, hp_ap, hhp, hs_ap, hhs, ci, T, Tp, has_samp, save_halo):
            cb_ = cbuf[ci]; hcb = Hcbuf[ci]; ac = cacc[ci]; hac = Hcacc[ci]
            Hh = K - 1
            if Tp > 0:
                cp("act", cb_[:, Hh:Hh + Tp], pacc[:, 0:Tp], [hpacc], [hcb])
                cp("act", cb_[:, 0:Hh], hp_ap, [hhp], [hcb])
                if save_halo:
                    cp("act", hp_ap, cb_[:, Tp:Tp + Hh], [hcb], [hhp])
                act(ac[:, 0:Tp], cb_[:, Hh:Hh + Tp], AF.Identity, [hcb, HP], [hac], bias=b_ap, scale=w_ap[:, Hh:Hh + 1])
                for j in range(Hh - 1, -1, -1):
                    stt("dve", ac[:, 0:Tp], cb_[:, j:j + Tp], w_ap[:, j:j + 1], ac[:, 0:Tp], ALU.mult, ALU.add, [hcb, HP, hac], [hac])
            if has_samp:
                W = Hh + LS
                cbs = cb_[:, SOFF:SOFF + 16 * W].rearrange("p (s t) -> p s t", t=W)
                acs = ac[:, Tp:Tp + 128].rearrange("p (s t) -> p s t", t=LS)
                cp("act", cbs[:, :, Hh:W], pacc[:, Tp:Tp + 128].rearrange("p (s t) -> p s t", t=LS), [hpacc], [hcb])
                cp("act", cbs[:, :, 0:Hh], hs_ap.rearrange("p (s t) -> p s t", t=Hh), [hhs], [hcb])
                act(acs, cbs[:, :, Hh:W], AF.Identity, [hcb, HP], [hac], bias=b_ap, scale=w_ap[:, Hh:Hh + 1])
                for j in range(Hh - 1, -1, -1):
                    stt("dve", acs, cbs[:, :, j:j + LS], w_ap[:, j:j + 1], acs, ALU.mult, ALU.add, [hcb, HP, hac], [hac])
            return ac, hac

        for ti, tile in enumerate(tiles):
            last_tile = (ti == len(tiles) - 1)
            cur_ti[0] = ti
            offs = []
            o = 0
            for s_ in tile:
                offs.append(o)
                o += s_["cfg"].L
            T = o
            has_samp = tile[-1]["kind"] == "samp"
            Tp = T - 128 if has_samp else T
            sonly = bool(tile[0].get("sonly", False))
            base = 32 * TMAX
            ygT_ap, HYG = AR.view("ygT", 0, 32 * TMAX)
            ygT = ygT_ap.rearrange("p (k t) -> p k t", k=32)
            xtok_ap, HXT = AR.view("x_tok", base, 3 * 2048)
            x_tok = xtok_ap.rearrange("p (s c) -> p s c", s=3)
            xTall_ap, HXTA = AR.view("xT_all", base + 6144, 16 * TMAX)
            xT_all = xTall_ap.rearrange("p (b t) -> p b t", b=16)
            bct_ap, HBCT = AR.view("BCT", base + 12288, 4 * TMAX)
            BCT = bct_ap.rearrange("p (b t) -> p b t", b=4)
            btok_ap, HBT = AR.view("B_tok", base + 13824, 3 * 256)
            B_tok = btok_ap.rearrange("p (s c) -> p s c", s=3)
            xdt, HXDT = AR.view("xdt", base + 14592, 2048)
            xtail, HXTL = AR.view("xtail", base + 16640, 2048)
            mt4_ap, HMT4 = AR.view("MT4", base + 18688, 512)
            MT4 = mt4_ap.rearrange("p (j t) -> p j t", j=4)
            cmt_ap, HCMT = AR.view("CmT", base + 19200, 256)
            CmT = cmt_ap.rearrange("p (g t) -> p g t", g=2)
            bm_ap, HBM = AR.view("Bm", base + 19456, 256)
            Bm = bm_ap.rearrange("p (g t) -> p g t", g=2)
            S0T, HS0T = AR.view("S0T", base + 19712, 2048)

            for si, s_ in enumerate(tile):
                L = s_["cfg"].L
                S.dma("sp", BIG[si][0:L, :], s_["src"], writes=[HBIG[si]])
                norm_T(si, L, offs[si], n1w, xnT, HXN)

            if last_tile:
                for q in range(5):
                    et = etmp[q % 2]; het = Hetmp[q % 2]
                    S.dma("sp", et[0:48, :], st_conv[:, q * 512:(q + 1) * 512], writes=[het])
                    for j in range(4):
                        tr(PB[2][:, j * 128:j * 128 + 48], et[0:48, j * 128:(j + 1) * 128], identf[0:48, 0:48], [het, HC], [HPB[2]])
                    cp("dve", halo_s[:, 4 * q:4 * q + 4, :], PB[2][:].rearrange("p (j l) -> p j l", j=4)[:, :, 0:48], [HPB[2]], [Hhs])
                S.dma("sp", etmp[0][:, 0:128], st_n[:, :], writes=[Hetmp[0]])
                tr(PB[2][:, 0:128], etmp[0][:, 0:128], identf[:, :], [Hetmp[0], HC], [HPB[2]])
                cp("dve", nall[:].rearrange("p s h -> p (s h)"), PB[2][:, 0:128], [HPB[2]], [Hnall])
                S.dma("sp", pstage[0:16, 0:8], st_m[:, :], writes=[Hpst])
                tr(PB[2][0:8, 0:16], pstage[0:16, 0:8], identf[0:16, 0:16], [Hpst, HC], [HPB[2]])
                cp("dve", mall[:, :], PB[2][0:8, 0:16], [HPB[2]], [Hmall])
                offp = offs[1]; offS = offs[2]
                cp("dve", xsel[:, :, 0:3], xnT[:, :, offp + 125:offp + 128], [HXN], [Hxsel])
                for kc in range(KC):
                    cp("dve", xsel[:, kc, 3:51].rearrange("p (s t) -> p s t", t=3),
                       xnT[:, kc, offS:offS + 128].rearrange("p (s t) -> p s t", t=LS)[:, :, 5:8], [HXN], [Hxsel])

            for wb in range(5):
                wt, hw = wload("in", XBC0 + wb * 512, 512, 0, KC)
                for sbk in range(4):
                    blk = wb * 4 + sbk
                    pacc = PB[2 + blk % 3]; hpacc = HPB[2 + blk % 3]
                    for kc in range(KC):
                        mm(pacc[:, 0:T], wt[:, kc, sbk * 128:(sbk + 1) * 128], xnT[:, kc, 0:T], kc == 0, kc == KC - 1, [hw, HXN], [hpacc])
                    ac, hac = conv_block(pacc, hpacc, 4, cw[:, blk, :], cbi[:, blk:blk + 1], halo_p[:, blk, :], Hhp,
                                         halo_s[:, blk, :], Hhs, blk % 2, T, Tp, has_samp, not last_tile)
                    if blk < 16:
                        act(xT_all[:, blk, 0:T], ac[:, 0:T], AF.Silu, [hac], [HXTA])
                    else:
                        act(BCT[:, blk - 16, 0:T], ac[:, 0:T], AF.Silu, [hac], [HBCT])
                if last_tile:
                    for kc in range(KC):
                        mm(PB[5][0:51, :], xsel[:, kc, 0:51], wt[:, kc, :], kc == 0, kc == KC - 1, [hw, Hxsel], [HPB[5]])
                    cp("act", crow[0:51, :], PB[5][0:51, :], [HPB[5]], [Hcrow])
                    for r_ in range(3):
                        S.dma("pool", p_conv[r_:r_ + 1, wb * 512:(wb + 1) * 512], crow[r_:r_ + 1, :], reads=[Hcrow], is_out=True)
                    S.dma("pool", s_conv[:, wb * 512:(wb + 1) * 512], crow[3:51, :], reads=[Hcrow], is_out=True)
            for si, s_ in enumerate(tile):
                L = s_["cfg"].L; off = offs[si]
                for half in range(2):
                    for j in range(8):
                        tr(TB[0:L, j * 128:(j + 1) * 128], xT_all[:, half * 8 + j, off:off + L], identb[:, :], [HXTA, HC], [HTB])
                    cp("dve", x_tok[0:L, si, half * 1024:(half + 1) * 1024], TB[0:L, :], [HTB], [HXT])
                for g in range(2):
                    tr(TB[0:L, g * 128:(g + 1) * 128], BCT[:, g, off:off + L], identb[:, :], [HBCT, HC], [HTB])
                cp("dve", B_tok[0:L, si, :], TB[0:L, 0:256], [HTB], [HBT])

            ptmp_ap, Hptmp = AR.view("ptmp", base + 6144, 1024)
            ptmp = ptmp_ap.bitcast(F32)
            mt4b_ap, HMT4b = AR.view("MT4b", base + 7168, 512)
            MT4b = mt4b_ap.rearrange("p (j t) -> p j t", j=4)
            wt, hw = wload("in", DT0, 32, 0, KC)
            for si, s_ in enumerate(tile):
                L = s_["cfg"].L; off = offs[si]
                for kc in range(KC):
                    mm(PB[si][0:L, 0:32], xnT[:, kc, off:off + L], wt[:, kc, 0:32], kc == 0, kc == KC - 1, [hw, HXN], [HPB[si]])
                tt("dve", sms[si][0:L, 0:32], PB[si][0:L, 0:32], dtb[0:L, :], ALU.add, [HPB[si], HP], [Hsms[si]])

            for si, s_ in enumerate(tile):
                c = s_["cfg"]; L = c.L; nseq = c.nseq; off = offs[si]
                samp = s_["kind"] == "samp"
                m_ = sms[si]; hm = Hsms[si]
                xx = m_[0:L, 0:32]; na = m_[0:L, 32:64]; l1 = m_[0:L, 64:96]; dt = m_[0:L, 96:128]; a_ = m_[0:L, 128:160]
                cum = m_[0:L, 160:192]; ncum = m_[0:L, 192:224]; ecum = m_[0:L, 224:256]; tail = m_[0:L, 256:288]
                stt("dve", na, xx, -1.0, xx, ALU.mult, ALU.min, [hm], [hm])
                act(na, na, AF.Exp, [hm], [hm])
                act(l1, na, AF.Ln, [hm], [hm], bias=1.0)
                stt("dve", dt, xx, 0.0, l1, ALU.max, ALU.add, [hm], [hm])
                tt("dve", a_, dt, Aneg[0:L, :], ALU.mult, [hm, HP], [hm])
                mm(PB[0][0:L, 0:32], c.U, a_, True, True, [hm, HC], [HPB[0]])
                mm(PB[0][0:L, 32:64], c.SS, a_, True, True, [hm, HC], [HPB[0]])
                mm(PB[0][0:32, 64:64 + nseq], a_, c.sel, True, True, [hm, HC], [HPB[0]])
                cp("dve", cum, PB[0][0:L, 0:32], [HPB[0]], [hm])
                ts("dve", ncum, cum, -1.0, ALU.mult, [hm], [hm])
                act(ecum, PB[0][0:L, 0:32], AF.Exp, [HPB[0]], [hm])
                tt("dve", tail, PB[0][0:L, 32:64], cum, ALU.subtract, [HPB[0], hm], [hm])
                act(tail, tail, AF.Exp, [hm], [hm])
                tt("dve", tail, tail, dt, ALU.mult, [hm], [hm])
                etot = smT[0:32, 0:nseq]
                act(etot, PB[0][0:32, 64:64 + nseq], AF.Exp, [HPB[0]], [HsmT])
                for j in range(16):
                    mm(PB[1][:, j * nseq:(j + 1) * nseq], Eexp[:, j * 128:(j + 1) * 128], etot, True, True, [HC, HsmT], [HPB[1]])
                cp("dve", decS[:, :, 0:nseq], PB[1][:, 0:16 * nseq].rearrange("p (j s) -> p j s", s=nseq), [HPB[1]], [HdecS])
                if not sonly:
                    tt("dve", xdt[0:L, :].rearrange("p (h q) -> p h q", q=64), x_tok[0:L, si, :].rearrange("p (h q) -> p h q", q=64),
                       dt.unsqueeze(2).to_broadcast([L, 32, 64]), ALU.mult, [HXT, hm], [HXDT])
                tt("dve", xtail[0:L, :].rearrange("p (h q) -> p h q", q=64), x_tok[0:L, si, :].rearrange("p (h q) -> p h q", q=64),
                   tail.unsqueeze(2).to_broadcast([L, 32, 64]), ALU.mult, [HXT, hm], [HXTL])
                if not sonly:
                    for g in range(2):
                        mm(PB[2][0:L, g * 128:g * 128 + L], BCT[:, g, off:off + L], BCT[:, 2 + g, off:off + L], True, True, [HBCT], [HPB[2]])
                    cp("act", CB[0:L, :, 0:L], PB[2][0:L, 0:256].rearrange("p (g t) -> p g t", g=2)[:, :, 0:L], [HPB[2]], [HCB])
                    PBC = ((PB[3], HPB[3]), (PB[6], HPB[6]))
                    MTB = ((MT4, HMT4), (MT4b, HMT4b))

                    def px_bcast(hq):
                        pbc, hpbc = PBC[hq % 2]
                        if L == 128:
                            mm(pbc[0:L, :].rearrange("p (j t) -> p j t", j=4)[:, :, 0:L], identb[0:L, 0:L], c.negm, True, False, [HC], [hpbc])
                        else:
                            for j in range(4):
                                mm(pbc[0:L, j * 128:j * 128 + L], identb[0:L, 0:L], c.negm1, j == 0, False, [HC], [hpbc])
                        for j in range(4):
                            h = 4 * hq + j
                            mm(pbc[0:L, j * 128:j * 128 + L], a_[:, h:h + 1].to_broadcast([L, L]), c.U, False, j == 3, [hm, HC], [hpbc])

                    def px_elem(hq):
                        pbc, hpbc = PBC[hq % 2]
                        mt, hmt = MTB[hq % 2]
                        g = hq // 4
                        et = etmp[ectr[0] % 2]; het = Hetmp[ectr[0] % 2]; ectr[0] += 1
                        for j in range(4):
                            h = 4 * hq + j
                            act(et[0:L, j * 128:j * 128 + L], pbc[0:L, j * 128:j * 128 + L], AF.Exp, [hpbc, hm], [het], bias=ncum[:, h:h + 1])
                        tt("dve", mt[0:L, :, 0:L], et[0:L, :].rearrange("p (j t) -> p j t", j=4)[:, :, 0:L],
                           CB[0:L, g, 0:L].unsqueeze(1).to_broadcast([L, 4, L]), ALU.mult, [het, HCB], [hmt])

                    def px_py(hq):
                        mt, hmt = MTB[hq % 2]
                        for j in range(4):
                            h = 4 * hq + j
                            hh = h % 16
                            py = PB[4 + hh // 8]; hpy = HPB[4 + hh // 8]
                            mm(py[0:L, (hh % 8) * 64:(hh % 8) * 64 + 64], mt[0:L, j, 0:L], xdt[0:L, h * 64:(h + 1) * 64], True, True, [hmt, HXDT], [hpy])

                    def px_evac(g):
                        for b2 in range(2):
                            cols = slice(g * 1024 + b2 * 512, g * 1024 + b2 * 512 + 512)
                            cp("act", BIG[si][0:L, cols], PB[4 + b2][0:L, :], [HPB[4 + b2]], [HBIG[si]])
                            h0 = g * 16 + b2 * 8
                            tt("dve", ptmp[0:L, :].rearrange("p (h q) -> p h q", q=64), x_tok[0:L, si, cols].rearrange("p (h q) -> p h q", q=64),
                               Dbc[0:L, h0:h0 + 8].unsqueeze(2).to_broadcast([L, 8, 64]), ALU.mult, [HXT, HP], [Hptmp])
                            tt("dve", BIG[si][0:L, cols], BIG[si][0:L, cols], ptmp[0:L, :], ALU.add, [HBIG[si], Hptmp], [HBIG[si]])

                    px_bcast(0)
                    for hq in range(8):
                        if hq + 1 < 8:
                            px_bcast(hq + 1)
                        px_elem(hq)
                        px_py(hq)
                        if hq % 4 == 3:
                            px_evac(hq // 4)
                PYI = ((PB[4], HPB[4]), (PB[5], HPB[5]), (PB[2], HPB[2]), (PB[3], HPB[3]))
                if samp:
                    wsl_i = wctr[0] % 2
                    HSalt = [H("Salt0"), H("Salt1")]
                    S.alias([HWSL[wsl_i]], HSalt)
                    Salt = WSL[wsl_i][:].rearrange("p k c -> p (k c)")[:, 0:4096].bitcast(F32).rearrange("p (j n) -> p j n", j=16)
                def ssd_loads(sq):
                    Sb_, HSb_ = (Salt, HSalt) if sq % 2 == 1 else (Sst, HSg)
                    for g_ in range(2):
                        S.dma("sp", Sb_[:, g_ * 8:g_ * 8 + 8, :], st_ssd[sq, g_ * 1024:(g_ + 1) * 1024, :].rearrange("(j p) n -> p j n", p=128),
                              writes=[HSb_[g_]])
                if samp:
                    ssd_loads(0)
                for seq in range(nseq):
                    Sb, HSb = (Salt, HSalt) if (samp and seq % 2 == 1) else (Sst, HSg)
                    if samp:
                        if seq + 1 < nseq:
                            ssd_loads(seq + 1)
                        tt("dve", CmT[:, :, :], BCT[:, 2:4, off:off + L], blkm[:, seq, :].unsqueeze(1).to_broadcast([128, 2, 128]), ALU.mult,
                           [HBCT, HC], [HCMT])
                        ts("dve", Bm[0:L, :, :].rearrange("p g t -> p (g t)"), B_tok[0:L, si, :], sel2[0:L, seq:seq + 1], ALU.mult, [HBT, HC], [HBM])
                    for g in range(2):
                        for q in range(2):
                            if sonly:
                                break
                            for jj in range(4):
                                j = g * 8 + q * 4 + jj
                                tr(PB[6][:, jj * 128:(jj + 1) * 128], Sb[:, j, :], identf[:, :], [HSb[g], HC], [HPB[6]])
                            c0 = (g * 8 + q * 4) * 128
                            cp("act", S0T[:, c0:c0 + 512], PB[6][:, :], [HPB[6]], [HS0T])
                        lhsC = CmT[:, g, 0:L] if samp else BCT[:, 2 + g, off:off + L]
                        hlc = HCMT if samp else HBCT
                        for b2 in range(2):
                            if sonly:
                                break
                            pyi, hpyi = PYI[g * 2 + b2]
                            mm(pyi[0:L, :], lhsC, S0T[:, g * 1024 + b2 * 512:g * 1024 + b2 * 512 + 512], seq == 0, seq == nseq - 1, [hlc, HS0T], [hpyi])
                        rhsB = Bm[0:L, g, :] if samp else B_tok[0:L, si, g * 128:(g + 1) * 128]
                        hrb = HBM if samp else HBT
                        for jj in range(8):
                            j = g * 8 + jj
                            mm(PB[jj // 4][:, (jj % 4) * 128:(jj % 4) * 128 + 128], xtail[0:L, j * 128:(j + 1) * 128], rhsB, True, True, [HXTL, hrb], [HPB[jj // 4]])
                        tt("dve", Sb[:, g * 8:g * 8 + 8, :], Sb[:, g * 8:g * 8 + 8, :], decS[:, g * 8:g * 8 + 8, seq:seq + 1].to_broadcast([128, 8, 128]),
                           ALU.mult, [HSb[g], HdecS], [HSb[g]])
                        for q in range(2):
                            tt("dve", Sb[:, g * 8 + q * 4:g * 8 + q * 4 + 4, :], Sb[:, g * 8 + q * 4:g * 8 + q * 4 + 4, :],
                               PB[q][:, :].rearrange("p (j n) -> p j n", j=4), ALU.add, [HSb[g], HPB[q]], [HSb[g]])
                        if samp:
                            S.dma("sp", s_ssd[seq, g * 1024:(g + 1) * 1024, :].rearrange("(j p) n -> p j n", p=128), Sb[:, g * 8:g * 8 + 8, :],
                                  reads=[HSb[g]], is_out=True)
                for g in range(2):
                    for b2 in range(2):
                        if sonly:
                            break
                        pyi, hpyi = PYI[g * 2 + b2]
                        cols = slice(g * 1024 + b2 * 512, g * 1024 + b2 * 512 + 512)
                        h0 = g * 16 + b2 * 8
                        et = etmp[ectr[0] % 2]; het = Hetmp[ectr[0] % 2]; ectr[0] += 1
                        tt("dve", et[0:L, :].rearrange("p (h q) -> p h q", q=64), pyi[0:L, :].rearrange("p (h q) -> p h q", q=64),
                           ecum[:, h0:h0 + 8].unsqueeze(2).to_broadcast([L, 8, 64]), ALU.mult, [hpyi, hm], [het])
                        tt("dve", BIG[si][0:L, cols], BIG[si][0:L, cols], et[0:L, :], ALU.add, [HBIG[si], het], [HBIG[si]])
                if samp:
                    S.alias(HSalt, [HWSL[wsl_i]])
                if s_.get("last"):
                    S.dma("pool", p_ssd.rearrange("(j p) n -> p j n", p=128), Sst[:], reads=HSg, is_out=True)
                dbgstage("yssd", si, s_)

            if not sonly:
                for wb in range(4):
                    wt, hw = wload("in", Z0 + wb * 512, 512, 0, KC)
                    for si, s_ in enumerate(tile):
                        L = s_["cfg"].L; off = offs[si]
                        pz = PB[(wb % 2) * 3 + si]; hpz = HPB[(wb % 2) * 3 + si]
                        for kc in range(KC):
                            mm(pz[0:L, :], xnT[:, kc, off:off + L], wt[:, kc, :], kc == 0, kc == KC - 1, [hw, HXN], [hpz])
                        et = etmp[ectr[0] % 2]; het = Hetmp[ectr[0] % 2]; ectr[0] += 1
                        act(et[0:L, :], pz[0:L, :], AF.Silu, [hpz], [het])
                        tt("dve", BIG[si][0:L, wb * 512:(wb + 1) * 512], BIG[si][0:L, wb * 512:(wb + 1) * 512], et[0:L, :], ALU.mult,
                           [HBIG[si], het], [HBIG[si]])
                for si, s_ in enumerate(tile):
                    norm_T(si, s_["cfg"].L, offs[si], snw, ygT[:, 0:16, :], HYG)

            qT_ap, HQT = AR.view("qT", base, 8 * TMAX)
            qT = qT_ap.rearrange("p (h t) -> p h t", h=8)
            kT_ap, HKT = AR.view("kT", base + 3072, 8 * TMAX)
            kT = kT_ap.rearrange("p (h t) -> p h t", h=8)
            ktok_ap, HKTOK = AR.view("k_tok", base + 6144, 3 * 1024)
            k_tok = ktok_ap.rearrange("p (s c) -> p s c", s=3)
            vtok_ap, HVT = AR.view("v_tok", base + 9216, 3 * 2048)
            v_tok = vtok_ap.rearrange("p (s c) -> p s c", s=3)
            wt_ap, HWT = AR.view("WT", base + 15360, 1024)
            WT = wt_ap.rearrange("p (h t) -> p h t", h=8)
            qm_ap, HQM = AR.view("qm", base + 16384, 1024)
            qm = qm_ap.rearrange("p (h t) -> p h t", h=8)
            ks, HKS = AR.view("ks", base + 17408, 1024)
            csc_ap0, HCsc0 = AR.view("Csc0", base + 18432, 1024)
            csc_ap1, HCsc1 = AR.view("Csc1", base + 19456, 1024)
            HCsch = [HCsc0, HCsc1]
            Csc = arena_t[:, base + 18432:base + 20480].rearrange("p (h v) -> p h v", h=8)
            ksall, HKSA = AR.view("ksall", base + 20480, 1024)
            wt, hw = wload("in", IF0, 16, 0, KC)
            for si, s_ in enumerate(tile):
                L = s_["cfg"].L; off = offs[si]
                for kc in range(KC):
                    mm(PB[si][0:L, 0:16], xnT[:, kc, off:off + L], wt[:, kc, 0:16], kc == 0, kc == KC - 1, [hw, HXN], [HPB[si]])
                tt("dve", sms[si][0:L, 288:296], PB[si][0:L, 0:8], ibb[0:L, :], ALU.add, [HPB[si], HP], [Hsms[si]])
                tt("dve", sms[si][0:L, 296:304], PB[si][0:L, 8:16], fbb[0:L, :], ALU.add, [HPB[si], HP], [Hsms[si]])

            def ml_prelude(si, s_):
                c = s_["cfg"]; L = c.L; nseq = c.nseq; Lseq = c.Lseq
                samp = s_["kind"] == "samp"
                m_ = sms[si]; hm = Hsms[si]
                ig = m_[0:L, 288:296]; fx = m_[0:L, 296:304]; nf = m_[0:L, 304:312]; l2 = m_[0:L, 312:320]; lf = m_[0:L, 320:328]
                g_ = m_[0:L, 328:336]; sk = m_[0:L, 336:344]; thr = m_[0:L, 344:352]; sks = m_[0:L, 352:360]
                P6 = PB[6]; H6 = HPB[6]
                stt("dve", nf, fx, -1.0, fx, ALU.mult, ALU.min, [hm], [hm])
                act(nf, nf, AF.Exp, [hm], [hm])
                act(l2, nf, AF.Ln, [hm], [hm], bias=1.0)
                stt("dve", lf, fx, 0.0, l2, ALU.min, ALU.subtract, [hm], [hm])
                mm(P6[0:L, 0:8], c.U, lf, True, True, [hm, HC], [H6])
                yield
                tt("dve", g_, ig, P6[0:L, 0:8], ALU.subtract, [hm, H6], [hm])
                mm(P6[0:8, 32:32 + L], g_, identf[0:L, 0:L], True, True, [hm, HC], [H6])
                mm(P6[0:8, 160:160 + L], lf, c.U, True, True, [hm, HC], [H6])
                mm(P6[0:8, 288:288 + nseq], lf, c.sel, True, True, [hm, HC], [H6])
                yield
                if samp:
                    m0T = mall[:, 0:16]; mnT = mnewT[:, 0:16]; hm0 = Hmall
                else:
                    m0T = mstT[:, 0:1]; mnT = mstT[:, 0:1]; hm0 = Hm
                cmax = smT[0:8, 0:nseq]; cc = smT[0:8, 16:16 + nseq]; decT = smT[0:8, 32:32 + nseq]
                skT = smT[0:8, 128:128 + L]; thrT = smT[0:8, 256:256 + L]; tmp8 = smT[0:8, 384:384 + L]
                decexp = smT[0:8, 512:512 + 8 * nseq]
                gTp = P6[0:8, 32:32 + L]; FTp = P6[0:8, 160:160 + L]
                S.op("dve", (lambda o_, i_: (lambda e: e.tensor_reduce(out=o_, in_=i_, axis=AX.X, op=ALU.max)))(
                    cmax, gTp.rearrange("p (s t) -> p s t", t=Lseq)), reads=[H6], writes=[HsmT])
                tt("dve", cc, cmax, m0T, ALU.max, [HsmT, hm0], [HsmT])
                tt("dve", decT, m0T, cc, ALU.subtract, [HsmT, hm0], [HsmT])
                act(decT, decT, AF.Exp, [HsmT], [HsmT])
                tt("dve", mnT, P6[0:8, 288:288 + nseq], cc, ALU.add, [H6, HsmT], [hm0])
                tt("dve", tmp8.rearrange("p (s t) -> p s t", t=Lseq), gTp.rearrange("p (s t) -> p s t", t=Lseq),
                   cc.unsqueeze(2).to_broadcast([8, nseq, Lseq]), ALU.subtract, [H6, HsmT], [HsmT])
                act(skT, tmp8, AF.Exp, [HsmT], [HsmT])
                stt("dve", tmp8.rearrange("p (s t) -> p s t", t=Lseq), FTp.rearrange("p (s t) -> p s t", t=Lseq), -1.0,
                    cc.unsqueeze(2).to_broadcast([8, nseq, Lseq]), ALU.mult, ALU.subtract, [H6, HsmT], [HsmT])
                act(thrT, tmp8, AF.Exp, [HsmT], [HsmT])
                tt("dve", decexp.rearrange("p (h s) -> p h s", s=nseq), decT.unsqueeze(1).to_broadcast([8, 8, nseq]),
                   identf[0:8, 0:8].unsqueeze(2).to_broadcast([8, 8, nseq]), ALU.mult, [HsmT, HC], [HsmT])
                mm(P6[0:L, 8:16], skT, identf[0:8, 0:8], True, True, [HsmT, HC], [H6])
                mm(P6[0:L, 16:24], thrT, identf[0:8, 0:8], True, True, [HsmT, HC], [H6])
                mm(P6[:, 320:320 + 8 * nseq], onesf[0:8, :], decexp, True, True, [HsmT, HC], [H6])
                yield
                cp("dve", sk, P6[0:L, 8:16], [H6], [hm])
                cp("dve", thr, P6[0:L, 16:24], [H6], [hm])
                ts("dve", sks, sk, 128.0 ** -0.5, ALU.mult, [hm], [hm])
                cp("dve", decbc3[:, si, :, 0:nseq], P6[:, 320:320 + 8 * nseq].rearrange("p (h s) -> p h s", s=nseq), [H6], [Hdecbc3[si]])

            def _chain():
                for si_, sx_ in enumerate(tile):
                    for _ in ml_prelude(si_, sx_):
                        yield
            preq = _chain()

            def pstep():
                next(preq, None)

            for which, c00, dstT_, hdst_ in ((0, Q0, qT, HQT), (1, K0, kT, HKT)):
                if sonly and which == 0:
                    continue
                for wb in range(2):
                    wt, hw = wload("in", c00 + wb * 512, 512, 0, KC)
                    for sbk in range(4):
                        h = wb * 4 + sbk
                        pacc = PB[3 + h % 3]; hpacc = HPB[3 + h % 3]
                        for kc in range(KC):
                            mm(pacc[:, 0:T], wt[:, kc, sbk * 128:(sbk + 1) * 128], xnT[:, kc, 0:T], kc == 0, kc == KC - 1, [hw, HXN], [hpacc])
                        cp("act", dstT_[:, h, 0:T], pacc[:, 0:T], [hpacc], [hdst_])
                        pstep()
            for si, s_ in enumerate(tile):
                L = s_["cfg"].L; off = offs[si]
                for h in range(8):
                    tr(TB[0:L, h * 128:(h + 1) * 128], kT[:, h, off:off + L], identb[:, :], [HKT, HC], [HTB])
                cp("dve", k_tok[0:L, si, :], TB[0:L, :], [HTB], [HKTOK])
            for wb in range(4):
                wt, hw = wload("in", V0 + wb * 512, 512, 0, KC)
                for si, s_ in enumerate(tile):
                    L = s_["cfg"].L; off = offs[si]
                    pv = PB[(wb % 2) * 3 + si]; hpv = HPB[(wb % 2) * 3 + si]
                    for kc in range(KC):
                        mm(pv[0:L, :], xnT[:, kc, off:off + L], wt[:, kc, :], kc == 0, kc == KC - 1, [hw, HXN], [hpv])
                    cp("act", v_tok[0:L, si, wb * 512:(wb + 1) * 512], pv[0:L, :], [hpv], [HVT])
                    pstep()
            for _ in preq:
                pass

            for si, s_ in enumerate(tile):
                c = s_["cfg"]; L = c.L; nseq = c.nseq; Lseq = c.Lseq; off = offs[si]
                samp = s_["kind"] == "samp"
                m_ = sms[si]; hm = Hsms[si]
                ig = m_[0:L, 288:296]; fx = m_[0:L, 296:304]; nf = m_[0:L, 304:312]; l2 = m_[0:L, 312:320]; lf = m_[0:L, 320:328]
                g_ = m_[0:L, 328:336]; sk = m_[0:L, 336:344]; thr = m_[0:L, 344:352]; sks = m_[0:L, 352:360]
                aden = m_[0:L, 360:368]; rd = m_[0:L, 368:376]; ssh = m_[0:L, 376:384]
                dn = m_[0:L, 0:8]; sksm = m_[0:L, 8:16]
                nsf = sm[:, 8:16]
                decbc = decbc3[:, si]; Hdecbc = Hdecbc3[si]
                if not sonly:
                    for hg in range(2):
                        pq = (PB[1], PB[6])[hg]; hpq = (HPB[1], HPB[6])[hg]
                        for j in range(4):
                            h = hg * 4 + j
                            mm(pq[0:L, j * 128:j * 128 + L], kT[:, h, off:off + L], qT[:, h, off:off + L], True, True, [HKT, HQT], [hpq])
                        for j in range(4):
                            h = hg * 4 + j
                            stt("dve", WT[0:L, h, 0:L], pq[0:L, j * 128:j * 128 + L], sks[:, h:h + 1], c.U, ALU.mult, ALU.mult, [hpq, hm, HC], [HWT])

                def nump(h):
                    return PB[2 + h // 2][0:L, (h % 2) * 256:(h % 2) * 256 + 256], HPB[2 + h // 2]
                tt("dve", ksall[0:L, :].rearrange("p (h d) -> p h d", h=8), k_tok[0:L, si, :].rearrange("p (h d) -> p h d", h=8),
                   sks.unsqueeze(2).to_broadcast([L, 8, 128]), ALU.mult, [HKTOK, hm], [HKSA])
                selb = sel2b[0:L, 0:16] if samp else onesb[0:L, 0:1]
                for h in range(8):
                    mm(PB[0][:, 256 + h * nseq:256 + (h + 1) * nseq], ksall[0:L, h * 128:(h + 1) * 128], selb, True, True, [HKSA, HC], [HPB[0]])
                cp("dve", nupd[:, :, 0:nseq], PB[0][:, 256:256 + 8 * nseq].rearrange("p (h s) -> p h s", s=nseq), [HPB[0]], [Hnupd])
                for h in range(8):
                    np_, hnp = nump(h)
                    if debug == "numinter" or sonly:
                        continue
                    mm(np_, WT[0:L, h, 0:L], v_tok[0:L, si, h * 256:(h + 1) * 256], h % 2 == 0, False, [HWT, HVT], [hnp])
                    mm(PB[0][0:L, 200 + h:201 + h], WT[0:L, h, 0:L], onesb[0:L, 0:1], h == 0, False, [HWT, HC], [HPB[0]])
                if samp:
                    wsl_i = wctr[0] % 2
                    HCalt = [H("Calt0"), H("Calt1")]
                    S.alias([HWSL[wsl_i]], HCalt)
                    Calt = WSL[wsl_i][:].rearrange("p k c -> p (k c)")[:, 0:4096].bitcast(F32).rearrange("p (h v) -> p h v", h=8)
                def ml_loads(sq):
                    Cb_, HCb_ = (Calt, HCalt) if sq % 2 == 1 else (Cst, HCh)
                    for hf_ in range(2):
                        S.dma("sp", Cb_[:, hf_ * 4:hf_ * 4 + 4, :], st_C[sq, hf_ * 4:hf_ * 4 + 4].rearrange("h d v -> d h v"), writes=[HCb_[hf_]])
                if samp:
                    ml_loads(0)
                for seq in range(nseq):
                    Cb, HCb = (Calt, HCalt) if (samp and seq % 2 == 1) else (Cst, HCh)
                    if samp:
                        if seq + 1 < nseq:
                            ml_loads(seq + 1)
                        n0 = nall[:, seq, :]; nnew_ap = nnew[:, seq, :]; hn0 = Hnall
                        tt("dve", qm[:, :, :], qT[:, :, off:off + L], blkm[:, seq, :].unsqueeze(1).to_broadcast([128, 8, 128]), ALU.mult,
                           [HQT, HC], [HQM])
                        ts("dve", ks[0:L, :], ksall[0:L, :], sel2[0:L, seq:seq + 1], ALU.mult, [HKSA, HC], [HKS])
                        ks_use = ks; hks_use = HKS
                    else:
                        n0 = nst[:, :]; nnew_ap = nst[:, :]; hn0 = Hn
                        ks_use = ksall; hks_use = HKSA
                    tt("dve", nsf, n0, decbc[:, :, seq], ALU.mult, [hn0, Hdecbc], [Hsm])
                    if not sonly:
                        cp("dve", nsc[:, :], nsf, [Hsm], [Hnsc])
                    for half in range(2):
                        hs_ = slice(half * 4, half * 4 + 4)
                        tt("dve", Cb[:, hs_, :], Cb[:, hs_, :], decbc[:, hs_, seq:seq + 1].to_broadcast([128, 4, 256]), ALU.mult,
                           [HCb[half], Hdecbc], [HCb[half]])
                        if not sonly:
                            cp("act", Csc[:, hs_, :], Cb[:, hs_, :], [HCb[half]], [HCsch[half]])
                        for h in range(half * 4, half * 4 + 4):
                            if sonly:
                                break
                            np_, hnp = nump(h)
                            lq = qm[:, h, 0:L] if samp else qT[:, h, off:off + L]
                            hlq = HQM if samp else HQT
                            st_flag = (debug == "numinter" and seq == 0)
                            mm(np_, lq, Csc[:, h, :], st_flag and h % 2 == 0, seq == nseq - 1 and h % 2 == 1, [hlq, HCsch[half]], [hnp])
                            mm(PB[0][0:L, 200 + h:201 + h], lq, nsc[:, h:h + 1], st_flag and h == 0, seq == nseq - 1 and h == 7, [hlq, Hnsc], [HPB[0]])
                        for jj in range(4):
                            h = half * 4 + jj
                            pc = (PB[1], PB[6])[jj // 2]; hpc = (HPB[1], HPB[6])[jj // 2]
                            mm(pc[:, (jj % 2) * 256:(jj % 2) * 256 + 256], ks_use[0:L, h * 128:(h + 1) * 128], v_tok[0:L, si, h * 256:(h + 1) * 256],
                               True, True, [hks_use, HVT], [hpc])
                        for q in range(2):
                            pc = (PB[1], PB[6])[q]; hpc = (HPB[1], HPB[6])[q]
                            h0 = half * 4 + q * 2
                            tt("dve", Cb[:, h0:h0 + 2, :], Cb[:, h0:h0 + 2, :], pc[:, :].rearrange("p (h v) -> p h v", h=2), ALU.add,
                               [HCb[half], hpc], [HCb[half]])
                        if samp:
                            S.dma("sp", s_C[seq, hs_].rearrange("h d v -> d h v"), Cb[:, hs_, :], reads=[HCb[half]], is_out=True)
                    tt("dve", nnew_ap, nsf, nupd[:, :, seq], ALU.add, [Hsm, Hnupd], [hn0])
                if samp:
                    S.alias(HCalt, [HWSL[wsl_i]])
                if not sonly:
                    cp("dve", dn, PB[0][0:L, 200:208], [HPB[0]], [hm])
                    stt("dve", aden, dn, -1.0, dn, ALU.mult, ALU.max, [hm], [hm])
                    tt("dve", aden, aden, thr, ALU.max, [hm], [hm])
                    S.op("dve", (lambda o_, i_: (lambda e: e.reciprocal(out=o_, in_=i_)))(rd, aden), reads=[hm], writes=[hm])
                    if debug == "numinter":
                        S.op("dve", (lambda o_: (lambda e: e.memset(o_, 1.0)))(rd), reads=[hm], writes=[hm])
                    for bk in range(4):
                        tt("dve", BIG[si][0:L, bk * 512:(bk + 1) * 512].rearrange("p (h v) -> p h v", h=2),
                           PB[2 + bk][0:L, :].rearrange("p (h v) -> p h v", h=2), rd[:, 2 * bk:2 * bk + 2].unsqueeze(2).to_broadcast([L, 2, 256]),
                           ALU.mult, [HPB[2 + bk], hm], [HBIG[si]])
                    dbgstage("hml", si, s_)
                    dbgstage("numinter", si, s_)
                    S.op("dve", (lambda o_: (lambda e: e.memset(o_, 0.0)))(ssh), reads=[hm], writes=[hm])
                    junk, Hjunk = AR.view("junk", JUNK0, 2048)
                    for h in range(8):
                        act(junk[0:L, 0:256], BIG[si][0:L, h * 256:(h + 1) * 256], AF.Square, [HBIG[si], hm], [Hjunk, hm], accum=ssh[:, h:h + 1])
                    rsqrt_mean(ssh, ssh, 256, L, [hm])
                    tt("dve", BIG[si][0:L, :].rearrange("p (h v) -> p h v", h=8), BIG[si][0:L, :].rearrange("p (h v) -> p h v", h=8),
                       ssh.unsqueeze(2).to_broadcast([L, 8, 256]), ALU.mult, [HBIG[si], hm], [HBIG[si]])
                if s_.get("last"):
                    S.dma("pool", p_C.rearrange("h d v -> d h v"), Cst[:], reads=HCh, is_out=True)
                    mm(PB[0][0:8, 0:128], nst[:, :], identf[:, :], True, True, [Hn, HC], [HPB[0]])
                    cp("dve", smT[0:8, 0:128], PB[0][0:8, 0:128], [HPB[0]], [HsmT])
                    S.dma("pool", p_n[:, :], smT[0:8, 0:128], reads=[HsmT], is_out=True)
                    S.dma("pool", p_m.rearrange("o h -> h o"), mstT[:, 0:1], reads=[Hm], is_out=True)
                if samp:
                    mm(PB[0][:, 0:128], nnew[:].rearrange("p s h -> p (s h)"), identf[:, :], True, True, [Hnall, HC], [HPB[0]])
                    cp("dve", etmp[0][:, 0:128], PB[0][:, 0:128], [HPB[0]], [Hetmp[0]])
                    S.dma("pool", s_n[:, :], etmp[0][:, 0:128], reads=[Hetmp[0]], is_out=True)
                    tr(PB[0][0:16, 0:8], mnewT[:, 0:16], identf[0:8, 0:8], [Hmall, HC], [HPB[0]])
                    cp("dve", pstage[0:16, 0:8], PB[0][0:16, 0:8], [HPB[0]], [Hpst])
                    S.dma("pool", s_m[:, :], pstage[0:16, 0:8], reads=[Hpst], is_out=True)

            if sonly:
                if ti == n_pre_tiles - 1:
                    S.dma("sp", pmk[:, :], pmask_d.partition_broadcast(128), writes=[Hpmk])
                    ts("dve", Sst[:].rearrange("p j n -> p (j n)"), Sst[:].rearrange("p j n -> p (j n)"), pmk[:, 0:1], ALU.mult, HSg + [Hpmk], HSg)
                    ts("dve", Cst[:].rearrange("p h v -> p (h v)"), Cst[:].rearrange("p h v -> p (h v)"), pmk[:, 0:1], ALU.mult, HCh + [Hpmk], HCh)
                    ts("dve", nst[:, :], nst[:, :], pmk[:, 0:1], ALU.mult, [Hn, Hpmk], [Hn])
                    ts("dve", mstT[:, :], mstT[:, :], pmk[0:8, 0:1], ALU.mult, [Hm, Hpmk], [Hm])
                    ts("dve", halo_p[:].rearrange("p b t -> p (b t)"), halo_p[:].rearrange("p b t -> p (b t)"), pmk[:, 0:1], ALU.mult, [Hhp, Hpmk], [Hhp])
                continue

            for wb in range(4):
                wt, hw = wload("in", O0 + wb * 512, 512, 0, KC)
                for si, s_ in enumerate(tile):
                    L = s_["cfg"].L; off = offs[si]
                    po = PB[(wb % 2) * 3 + si]; hpo = HPB[(wb % 2) * 3 + si]
                    for kc in range(KC):
                        mm(po[0:L, :], xnT[:, kc, off:off + L], wt[:, kc, :], kc == 0, kc == KC - 1, [hw, HXN], [hpo])
                    et = etmp[ectr[0] % 2]; het = Hetmp[ectr[0] % 2]; ectr[0] += 1
                    act(et[0:L, :], po[0:L, :], AF.Sigmoid, [hpo], [het])
                    tt("dve", BIG[si][0:L, wb * 512:(wb + 1) * 512], BIG[si][0:L, wb * 512:(wb + 1) * 512], et[0:L, :], ALU.mult,
                       [HBIG[si], het], [HBIG[si]])
            for si, s_ in enumerate(tile):
                norm_T(si, s_["cfg"].L, offs[si], mnw, ygT[:, 16:32, :], HYG, prescaled=True)

            hctr = 0
            for cb_i in range(4):
                wA, hwA = wload("out", cb_i * 512, 512, 0, 16)
                wB, hwB = wload("out", cb_i * 512, 512, 16, 16)
                for si, s_ in enumerate(tile):
                    L = s_["cfg"].L; off = offs[si]
                    hr = hres[0]; hhr = Hhres[0]
                    S.dma("sp", hr[0:L, :], s_["src"][:, cb_i * 512:(cb_i + 1) * 512], writes=[hhr])
                    po = PB[(cb_i % 2) * 3 + si]; hpo = HPB[(cb_i % 2) * 3 + si]
                    for kc in range(32):
                        w_, hw_ = (wA, hwA) if kc < 16 else (wB, hwB)
                        mm(po[0:L, :], ygT[:, kc, off:off + L], w_[:, kc % 16, :], kc == 0, kc == 31, [hw_, HYG], [hpo])
                    tt("dve", BIG[si][0:L, cb_i * 512:(cb_i + 1) * 512], po[0:L, :], hr[0:L, :], ALU.add, [hpo, hhr], [HBIG[si]])
            for si, s_ in enumerate(tile):
                dbgstage("hmid", si, s_)
            xsc_ap, Hxsc = AR.view("xsc", base, 4096)
            xsc = xsc_ap.bitcast(F32)
            for si, s_ in enumerate(tile):
                norm_T(si, s_["cfg"].L, offs[si], n2w, xnT, HXN, scaled_dst=(xsc, Hxsc))

            gT_ap, HGT = AR.view("gT", 0, 44 * TMAX)
            gT = gT_ap.rearrange("p (k t) -> p k t", k=44)
            fnw_ap, Hfnw = AR.view("fnw", 22528, 4096)
            fnw = fnw_ap.bitcast(F32)
            S.dma("sp", fnw[:, :], final_norm_w.partition_broadcast(128), writes=[Hfnw])
            if last_tile:
                fhs_ap, Hfs = AR.view("fhalo_s", 16896, 5632)
                fhalo_s = fhs_ap.bitcast(F32).rearrange("p (b t) -> p b t", b=88)
                for q in range(22):
                    et = etmp[q % 2]; het = Hetmp[q % 2]
                    S.dma("sp", et[0:32, :], st_ffn[:, q * 512:(q + 1) * 512], writes=[het])
                    for j in range(4):
                        tr(PB[2][:, j * 128:j * 128 + 32], et[0:32, j * 128:(j + 1) * 128], identf[0:32, 0:32], [het, HC], [HPB[2]])
                    cp("dve", fhalo_s[:, 4 * q:4 * q + 4, :], PB[2][:].rearrange("p (j l) -> p j l", j=4)[:, :, 0:32], [HPB[2]], [Hfs])
                offp = offs[1]; offS = offs[2]
                cp("dve", xsel[:, :, 0:2], xnT[:, :, offp + 126:offp + 128], [HXN], [Hxsel])
                for kc in range(KC):
                    cp("dve", xsel[:, kc, 2:34].rearrange("p (s t) -> p s t", t=2),
                       xnT[:, kc, offS:offS + 128].rearrange("p (s t) -> p s t", t=LS)[:, :, 6:8], [HXN], [Hxsel])
            else:
                fhalo_s = None; Hfs = None
            for jb in range(22):
                wgv, hwgv = wload("up", jb * 256, 512, 0, KC)
                for sbk in range(2):
                    j = jb * 2 + sbk
                    pg = PB[2 * (j % 3)]; hpg = HPB[2 * (j % 3)]
                    pv = PB[2 * (j % 3) + 1]; hpv = HPB[2 * (j % 3) + 1]
                    for kc in range(KC):
                        mm(pg[:, 0:T], wgv[:, kc, sbk * 128:(sbk + 1) * 128], xnT[:, kc, 0:T], kc == 0, kc == KC - 1, [hwgv, HXN], [hpg])
                    for kc in range(KC):
                        mm(pv[:, 0:T], wgv[:, kc, 256 + sbk * 128:256 + (sbk + 1) * 128], xnT[:, kc, 0:T], kc == 0, kc == KC - 1, [hwgv, HXN], [hpv])
                    gc, hgc = conv_block(pg, hpg, 3, fw[:, j, :], fbi[:, j:j + 1], fhalo_p[:, j, :], Hfp,
                                         fhalo_s[:, j, :] if last_tile else None, Hfs, 0, T, Tp, has_samp, not last_tile)
                    vc, hvc = conv_block(pv, hpv, 3, fw[:, 44 + j, :], fbi[:, 44 + j:45 + j], fhalo_p[:, 44 + j, :], Hfp,
                                         fhalo_s[:, 44 + j, :] if last_tile else None, Hfs, 1, T, Tp, has_samp, not last_tile)
                    act(gc[:, 0:T], gc[:, 0:T], AF.Silu, [hgc], [hgc])
                    tt("dve", gT[:, j, 0:T], gc[:, 0:T], vc[:, 0:T], ALU.mult, [hgc, hvc], [HGT])
                if last_tile:
                    for kc in range(KC):
                        mm(PB[6][0:34, :], xsel[:, kc, 0:34], wgv[:, kc, :], kc == 0, kc == KC - 1, [hwgv, Hxsel], [HPB[6]])
                    cp("act", crow[0:34, :], PB[6][0:34, :], [HPB[6]], [Hcrow])
                    for half, c0 in ((0, jb * 256), (1, DFF + jb * 256)):
                        S.dma("pool", p_ffn[:, c0:c0 + 256], crow[0:2, half * 256:half * 256 + 256], reads=[Hcrow], is_out=True)
                        S.dma("pool", s_ffn[:, c0:c0 + 256], crow[2:34, half * 256:half * 256 + 256], reads=[Hcrow], is_out=True)

            for cb_i in range(4):
                for kg in range(4):
                    wt, hw = wload("dn", cb_i * 512, 512, kg * 11, 11)
                    for si, s_ in enumerate(tile):
                        L = s_["cfg"].L; off = offs[si]
                        pd = PB[(cb_i % 2) * 3 + si]; hpd = HPB[(cb_i % 2) * 3 + si]
                        for kk in range(11):
                            kc = kg * 11 + kk
                            mm(pd[0:L, :], gT[:, kc, off:off + L], wt[:, kk, :], kc == 0, kc == 43, [hw, HGT], [hpd])
                for si, s_ in enumerate(tile):
                    L = s_["cfg"].L
                    pd = PB[(cb_i % 2) * 3 + si]; hpd = HPB[(cb_i % 2) * 3 + si]
                    tt("dve", BIG[si][0:L, cb_i * 512:(cb_i + 1) * 512], BIG[si][0:L, cb_i * 512:(cb_i + 1) * 512], pd[0:L, :], ALU.add,
                       [HBIG[si], hpd], [HBIG[si]])
            for si, s_ in enumerate(tile):
                L = s_["cfg"].L
                if s_["dst"] is None:
                    continue
                S.op("dve", (lambda o_: (lambda e: e.memset(o_, 0.0)))(sm[0:L, 2:3]), writes=[Hsm])
                junk, Hjunk = AR.view("junk", JUNK0, 2048)
                act(junk[0:L, :], BIG[si][0:L, :], AF.Square, [HBIG[si], Hsm], [Hjunk, Hsm], accum=sm[0:L, 2:3])
                rsqrt_mean(sm[0:L, 3:4], sm[0:L, 2:3], D, L, [Hsm])
                stt("dve", BIG[si][0:L, :], BIG[si][0:L, :], sm[0:L, 3:4], fnw[0:L, :], ALU.mult, ALU.mult, [HBIG[si], Hsm, Hfnw], [HBIG[si]])
                if not debug:
                    S.dma("pool", s_["dst"], BIG[si][0:L, :], reads=[HBIG[si]], is_out=True)

        S.finish()
        S.emit_all()
    return nc


_NC_CACHE = {}
PRE_LS = [128] * 7 + [126]
FULL_LS = [18] + [128] * 8


def _get_nc(pre_Ls, full_Ls):
    key = (tuple(pre_Ls), tuple(full_Ls))
    if key not in _NC_CACHE:
        _NC_CACHE[key] = build_nc(list(pre_Ls), list(full_Ls))
    return _NC_CACHE[key]


def make_in_map(h2, G, x_sample_c, st, w, npre, nfull):
    f = lambda a: np.ascontiguousarray(a, dtype=np.float32)
    xfull = G[0:nfull] if h2 == 0 else G[npre:npre + nfull]
    m = {
        "xpre": f(G[0:npre]), "xfull": f(xfull), "xs": f(x_sample_c.reshape(128, D)),
        "pmask": np.array([float(h2)], dtype=np.float32),
        "st_conv": f(st["state_ssd_conv"].reshape(48, 2560)), "st_ssd": f(st["state_ssd"].reshape(16, 2048, 128)),
        "st_C": f(st["state_mlstm_C"]), "st_n": f(st["state_mlstm_n"].reshape(128, 128)), "st_m": f(st["state_mlstm_m"]),
        "st_ffn": f(st["state_ffn_conv"].reshape(32, 2 * DFF)),
        "w_in": f(w["w_in"][0]), "w_out": f(w["w_out"][0]), "w_up": f(w["w_up"][0]), "w_down": f(w["w_down"][0]),
        "norm1_w": f(w["norm1_w"][0]), "norm2_w": f(w["norm2_w"][0]), "ssd_norm_w": f(w["ssd_norm_w"][0]),
        "ml_norm_w": f(w["ml_norm_w"][0]), "final_norm_w": f(w["final_norm_w"]),
        "ssd_conv_w": f(w["ssd_conv_w"][0]), "ssd_conv_b": f(w["ssd_conv_b"][0]), "ssd_dt_bias": f(w["ssd_dt_bias"][0]),
        "ssd_A_log": f(w["ssd_A_log"][0]), "ssd_D": f(w["ssd_D"][0]), "ml_i_bias": f(w["ml_i_bias"][0]),
        "ml_f_bias": f(w["ml_f_bias"][0]), "ffn_conv_w": f(w["ffn_conv_w"][0]), "ffn_conv_b": f(w["ffn_conv_b"][0]),
    }
    return m


def kernel(**inp):
    n = 8
    pre_Ls, full_Ls = PRE_LS, FULL_LS
    npre, nfull = sum(pre_Ls), sum(full_Ls)
    SEQ_ = inp["x_prompt"].shape[1]
    assert npre + nfull == NMETA + SEQ_
    nc = _get_nc(pre_Ls, full_Ls)
    in_maps = []
    for c in range(n):
        b, h2 = c // 2, c % 2
        sl = slice(16 * c, 16 * c + 16)
        st = {k: np.asarray(inp[k])[0, sl] for k in ("state_ssd_conv", "state_ssd", "state_mlstm_C", "state_mlstm_n",
                                                      "state_mlstm_m", "state_ffn_conv")}
        G = np.concatenate([np.asarray(inp["meta_tokens"], dtype=np.float32), np.asarray(inp["x_prompt"])[b]], 0)
        in_maps.append(make_in_map(h2, G, np.asarray(inp["x_sample"])[sl], st, inp, npre, nfull))
    res = run_bass_kernel_spmd(nc, in_maps, core_ids=list(range(n)))
    R = res.results
    half = SEQ_ // 2
    y_prompt = np.stack([np.concatenate([R[2 * b]["yp"][NMETA:NMETA + half], R[2 * b + 1]["yp"][NMETA + half - npre:]], 0)
                         for b in range(4)], 0)
    y_sample = np.concatenate([R[c]["ys"].reshape(16, 8, D) for c in range(n)], 0)
    p = lambda k, shp: np.stack([R[2 * b + 1][k].reshape(shp) for b in range(4)], 0)[None]
    s = lambda k, shp: np.concatenate([R[c][k].reshape((16,) + shp) for c in range(n)], 0)[None]
    outs = (y_prompt, y_sample,
            p("p_conv", (3, 2560)), p("p_ssd", (32, 64, 128)), p("p_C", (8, 128, 256)), p("p_n", (8, 128)), p("p_m", (8,)),
            p("p_ffn", (2, 2 * DFF)),
            s("s_conv", (3, 2560)), s("s_ssd", (32, 64, 128)), s("s_C", (8, 128, 256)), s("s_n", (8, 128)), s("s_m", (8,)),
            s("s_ffn", (2, 2 * DFF)))
    return tuple(np.ascontiguousarray(o, dtype=np.float32) for o in outs)
```
